# Optimizing a Trainium2 kernel written in Bass

```python
import jax, jax.numpy as jnp
from jax import lax
import numpy as np

D_MODEL = 4096
BATCH = 8
SEQ = 2048
DEPTH = 1
DEC_BATCH = 4
DEC_SEQ = 4096
PAST_LEN = 128

EXPAND = 2
MIX_WIDTH = EXPAND * D_MODEL
W_POOL = MIX_WIDTH // 2
W_CONV = MIX_WIDTH // 2
POOL_WINDOWS = (2, 4, 8, 16)
N_POOL_GROUPS = len(POOL_WINDOWS)
POOL_GROUP = W_POOL // N_POOL_GROUPS
CONV_WIDTH = 31
CONV_PAD = CONV_WIDTH // 2
N_IN = 2 * W_POOL + 3 * W_CONV + 2 * D_MODEL
SPLITS = tuple(np.cumsum([W_POOL, W_POOL, W_CONV, W_CONV, W_CONV, D_MODEL])[:].tolist())
RMS_EPS = 1e-6
LN_EPS = 1e-5

kernel_name = "hybrid_pool_conformer_gated_encoder"


def rmsnorm(x, g):
    xf = x.astype(jnp.float32)
    xf = xf * lax.rsqrt(jnp.mean(xf * xf, axis=-1, keepdims=True) + RMS_EPS)
    return xf.astype(x.dtype) * g


def layernorm(x, g, b):
    xf = x.astype(jnp.float32)
    mu = jnp.mean(xf, axis=-1, keepdims=True)
    xc = xf - mu
    var = jnp.mean(xc * xc, axis=-1, keepdims=True)
    return (xc * lax.rsqrt(var + LN_EPS)).astype(x.dtype) * g + b


def centred_mean(u, k):
    S = u.shape[1]
    c = jnp.cumsum(u.astype(jnp.float32), axis=1)
    c = jnp.pad(c, ((0, 0), (1, 0), (0, 0)))
    t = jnp.arange(S)
    lo = jnp.clip(t - k // 2, 0, S - 1)
    hi = jnp.clip(t + k - k // 2 - 1, 0, S - 1)
    s = jnp.take(c, hi + 1, axis=1) - jnp.take(c, lo, axis=1)
    cnt = (hi - lo + 1).astype(jnp.float32)[None, :, None]
    return (s / cnt).astype(u.dtype)


def pool_branch(u, w_pool, ls_pool):
    B, S, _ = u.shape
    ug = u.reshape(B, S, N_POOL_GROUPS, POOL_GROUP)
    pooled = jnp.stack([centred_mean(ug[:, :, gi, :], k) for gi, k in enumerate(POOL_WINDOWS)], axis=2)
    mixed = pooled - ug
    mixed = jnp.einsum('bsgc,gcd->bsgd', mixed, w_pool)
    return mixed.reshape(B, S, W_POOL) * ls_pool


def conv_branch(a, b, conv_w, conv_b, ln_g, ln_b):
    v = a * jax.nn.sigmoid(b)
    v = lax.conv_general_dilated(
        v, conv_w[:, None, :].astype(v.dtype), window_strides=(1,),
        padding=[(CONV_PAD, CONV_PAD)], dimension_numbers=('NWC', 'WIO', 'NWC'),
        feature_group_count=W_CONV) + conv_b
    v = layernorm(v, ln_g, ln_b)
    return jax.nn.silu(v)


def mixer_layer(x, g_norm, w_in, w_pool, ls_pool, w_proj_pool, conv_w, conv_b, ln_g, ln_b,
                w_proj_conv, b_proj_conv, w_out):
    h = rmsnorm(x, g_norm)
    p = jnp.einsum('bsd,dn->bsn', h, w_in)
    u_pool, z_pool, a_conv, b_conv, z_conv, g_pool, g_conv = jnp.split(p, SPLITS, axis=-1)
    y_pool = pool_branch(u_pool, w_pool, ls_pool) * jax.nn.silu(z_pool)
    y_pool = jnp.einsum('bsw,wd->bsd', y_pool, w_proj_pool)
    y_conv = conv_branch(a_conv, b_conv, conv_w, conv_b, ln_g, ln_b) * jax.nn.silu(z_conv)
    y_conv = jnp.einsum('bsw,wd->bsd', y_conv, w_proj_conv) + b_proj_conv
    merged = jax.nn.sigmoid(g_pool) * y_pool + jax.nn.sigmoid(g_conv) * y_conv
    return x + jnp.einsum('bsd,de->bse', merged, w_out)


def trunk(x, g_norm, w_in, w_pool, ls_pool, w_proj_pool, conv_w, conv_b, ln_g, ln_b,
          w_proj_conv, b_proj_conv, w_out, g_final):
    for l in range(DEPTH):
        x = mixer_layer(x, g_norm[l], w_in[l], w_pool[l], ls_pool[l], w_proj_pool[l], conv_w[l],
                        conv_b[l], ln_g[l], ln_b[l], w_proj_conv[l], b_proj_conv[l], w_out[l])
    return rmsnorm(x, g_final)


def setup_inputs(seed: int = 0) -> dict:
    key = jax.random.key(seed)
    ks = jax.random.split(key, 16)
    f32 = jnp.float32
    nrm = lambda k, shape, s: jax.random.normal(k, shape, f32) * s
    return {
        "x_prompt": nrm(ks[0], (BATCH, SEQ, D_MODEL), 1.0),
        "x_sample": nrm(ks[1], (DEC_BATCH, DEC_SEQ, D_MODEL), 1.0),
        "g_norm": 1.0 + nrm(ks[2], (DEPTH, D_MODEL), 0.02),
        "w_in": nrm(ks[3], (DEPTH, D_MODEL, N_IN), D_MODEL ** -0.5),
        "w_pool": nrm(ks[4], (DEPTH, N_POOL_GROUPS, POOL_GROUP, POOL_GROUP), POOL_GROUP ** -0.5),
        "ls_pool": 1.0 + nrm(ks[5], (DEPTH, W_POOL), 0.02),
        "w_proj_pool": nrm(ks[6], (DEPTH, W_POOL, D_MODEL), W_POOL ** -0.5),
        "conv_w": nrm(ks[7], (DEPTH, CONV_WIDTH, W_CONV), CONV_WIDTH ** -0.5),
        "conv_b": nrm(ks[8], (DEPTH, W_CONV), 0.02),
        "ln_g": 1.0 + nrm(ks[9], (DEPTH, W_CONV), 0.02),
        "ln_b": nrm(ks[10], (DEPTH, W_CONV), 0.02),
        "w_proj_conv": nrm(ks[11], (DEPTH, W_CONV, D_MODEL), W_CONV ** -0.5),
        "b_proj_conv": nrm(ks[12], (DEPTH, D_MODEL), 0.02),
        "w_out": nrm(ks[13], (DEPTH, D_MODEL, D_MODEL), D_MODEL ** -0.5),
        "g_final": 1.0 + nrm(ks[14], (D_MODEL,), 0.02),
    }


def reference(x_prompt, x_sample, g_norm, w_in, w_pool, ls_pool, w_proj_pool, conv_w, conv_b,
              ln_g, ln_b, w_proj_conv, b_proj_conv, w_out, g_final):
    y_prompt = trunk(x_prompt, g_norm, w_in, w_pool, ls_pool, w_proj_pool, conv_w, conv_b,
                     ln_g, ln_b, w_proj_conv, b_proj_conv, w_out, g_final)
    y_sample = trunk(x_sample, g_norm, w_in, w_pool, ls_pool, w_proj_pool, conv_w, conv_b,
                     ln_g, ln_b, w_proj_conv, b_proj_conv, w_out, g_final)
    return (y_prompt, y_sample)
```

```python
import contextlib
import numpy as np
import concourse.bass as bass
import concourse.mybir as mybir
from concourse.bass_utils import run_bass_kernel_spmd
from concourse.alu_op_type import AluOpType as ALU

F32 = mybir.dt.float32
BF16 = mybir.dt.bfloat16
F32R = mybir.dt.float32r
AF = mybir.ActivationFunctionType

D = 4096
KC = 32
T = 416
WIN = 448
NTC = 5
CHR = 2112
NCH = 2
ROWS = NCH * CHR
NG = ROWS // 128
NIN = 28672
SEG_U, SEG_ZP, SEG_A, SEG_B, SEG_ZC, SEG_GP, SEG_GC = range(7)
RMS_EPS = 1e-6
NCONV = 168
NPOOL = 12
LN_EPS = 1e-5
ENG = ('pe', 'act', 'dve', 'sp', 'pool')

O_RH = 0
O_BIG = O_RH + 28672
O_RA = O_BIG
O_RB = O_BIG + 53248
O_GF = O_BIG + 79872
O_TMP = O_GF + 16384
NTMP = 17
O_WR = O_TMP + NTMP * 1792 + 256
O_ONES = O_WR + 32768
O_IDENT = O_ONES + 512
O_IDF = O_IDENT + 256
O_PV = O_IDF + 512
O_CW = O_PV + 640
O_MK = O_CW + 3968
O_INV = O_MK + 1792
O_SM = O_INV + 4 * 1664
O_END = O_SM + 1024


class Buf:
    __slots__ = ('w', 'r')

    def __init__(self):
        self.w = None
        self.r = {}


class Tr:
    def __init__(self):
        self.ops = {e: [] for e in ENG}
        self.cnt = {e: 0 for e in ENG}
        self.waited = {e: {} for e in ENG}
        self.dcnt = {}

    def wait(self, eng, deps):
        best = {}
        for d in deps:
            if d is None:
                continue
            k, v = d
            if v > best.get(k, 0):
                best[k] = v
        for k, v in best.items():
            if k == eng and eng == 'pe':
                continue
            if self.waited[eng].get(k, 0) >= v:
                continue
            self.waited[eng][k] = v
            self.ops[eng].append(('w', k, v))

    def _deps(self, reads, writes, extra):
        deps = list(extra)
        for b in reads:
            deps.append(b.w)
        for b in writes:
            deps.append(b.w)
            deps.extend(b.r.items())
        return deps

    def _upd(self, me, reads, writes):
        for b in reads:
            if b.r.get(me[0], 0) < me[1]:
                b.r[me[0]] = me[1]
        for b in writes:
            b.w = me
            b.r = {}

    def op(self, eng, fn, reads=(), writes=(), extra=()):
        self.wait(eng, self._deps(reads, writes, extra))
        self.cnt[eng] += 1
        me = (eng, self.cnt[eng])
        self.ops[eng].append(('o', fn))
        self._upd(me, reads, writes)
        return me

    def dma(self, eng, fn, semkey, reads=(), writes=(), extra=()):
        self.wait(eng, self._deps(reads, writes, extra))
        self.dcnt[semkey] = self.dcnt.get(semkey, 0) + 16
        me = (semkey, self.dcnt[semkey])
        self.ops[eng].append(('d', fn, semkey))
        self._upd(me, reads, writes)
        return me


def build_nc(ntiles=NCH * NTC):
    nc = bass.Bass("TRN2", target_bir_lowering=False)
    dt = nc.dram_tensor
    x_in = dt("x_in", [ROWS, D], F32, kind="ExternalInput").ap()
    maskb = dt("maskb", [128, ROWS], F32, kind="ExternalInput").ap()
    w_in = dt("w_in", [112, 128, 2, 4096], F32, kind="ExternalInput").ap()
    w_pool = dt("w_pool", [4, 4, 128, 2048], F32, kind="ExternalInput").ap()
    w_pp = dt("w_pp", [16, 128, 2, 4096], F32, kind="ExternalInput").ap()
    w_pc = dt("w_pc", [16, 128, 2, 4096], F32, kind="ExternalInput").ap()
    w_out = dt("w_out", [8, 128, 4, 4096], F32, kind="ExternalInput").ap()
    gnb = dt("gnb", [128, D], F32, kind="ExternalInput").ap()
    gfb = dt("gfb", [128, D], F32, kind="ExternalInput").ap()
    pvec = dt("pvec", [128, 160], F32, kind="ExternalInput").ap()
    cwp = dt("cwp", [128, 992], F32, kind="ExternalInput").ap()
    cst = dt("cst", [128, 256], F32, kind="ExternalInput").ap()
    y = dt("y", [NCH * 2048, D], F32, kind="ExternalOutput").ap()
    hT_all = dt("hT_all", [KC, 128, ROWS], BF16, kind="Internal").ap()
    wq_in = dt("wq_in", [112, 128, 2, 4096], BF16, kind="Internal").ap()
    wq_pool = dt("wq_pool", [4, 4, 128, 2048], BF16, kind="Internal").ap()
    wq_pp = dt("wq_pp", [16, 128, 2, 4096], BF16, kind="Internal").ap()
    wq_pc = dt("wq_pc", [16, 128, 2, 4096], BF16, kind="Internal").ap()
    wq_out = dt("wq_out", [8, 128, 4, 4096], BF16, kind="Internal").ap()

    tr = Tr()

    with contextlib.ExitStack() as es:
        S = es.enter_context(nc.sbuf_tensor("S", [128, O_END // 4], F32))
        PS = es.enter_context(nc.psum_tensor("PS", [128, 8, 512], F32))
        sems = {}

        def sem(key):
            if key not in sems:
                sems[key] = es.enter_context(nc.semaphore("s_" + key))
            return sems[key]

        for e in ('pe', 'act', 'dve', 'pool'):
            sem(e)

        def view(off, shape, dtype):
            n = int(np.prod(shape))
            esz = 4 if dtype == F32 else 2
            ap = S[:, off // 4: off // 4 + (n * esz + 3) // 4]
            if dtype != F32:
                ap = ap.bitcast(dtype)
            if len(shape) == 2:
                ap = ap.rearrange("p (a b) -> p a b", b=shape[1])
            return ap

        def PSb(bank):
            return PS[:, bank, :].bitcast(BF16)

        hT = view(O_RH, [KC, WIN], BF16)
        cbuf = view(O_RA, [KC, T], F32)
        merged = view(O_RA, [KC, T], BF16)
        yp_in = view(O_RA + 26624, [KC, T], BF16)
        yc_in = view(O_RB, [KC, T], BF16)
        ybuf = [view(O_RA + 26624 + i * 16384, [D], F32) for i in range(3)] + [view(O_TMP + 8 * 1792, [D], F32)]
        gf = view(O_GF, [D], F32)
        tmpv = [view(O_TMP + i * 1792, [WIN], F32) for i in range(NTMP)]
        wslot = [view(O_WR + i * 8192, [4096], BF16) for i in range(4)]
        ones = view(O_ONES, [128], F32)
        ident = view(O_IDENT, [128], BF16)
        idf = view(O_IDF, [128], F32)
        pv = view(O_PV, [160], F32)
        cw = view(O_CW, [KC, 31], F32)
        mk = view(O_MK, [WIN], F32)
        inv = [view(O_INV + i * 1664, [T], F32) for i in range(4)]
        sm = view(O_SM, [256], F32)
        epsr = sm[:, 200:201]
        epsl = sm[:, 201:202]
        junkv = [view(O_TMP + 4 * 1792, [512], F32), view(O_TMP + 6 * 1792, [512], F32)]

        def pcol(j, i):
            return pv[:, j * 32 + i: j * 32 + i + 1]
        PV_LS, PV_CB, PV_LNG, PV_LNB, PV_BPC = range(5)

        b_hT = Buf()
        b_c = [Buf() for _ in range(KC)]
        b_yc = [Buf() for _ in range(KC)]
        b_yp = [Buf() for _ in range(KC)]
        b_mg = [Buf() for _ in range(KC)]
        b_y = [Buf() for _ in range(4)]
        b_gf = Buf()
        b_t = [Buf() for _ in range(NTMP)]
        b_w = [Buf() for _ in range(4)]
        b_bank = [Buf() for _ in range(8)]
        b_const = Buf()
        b_mk = Buf()
        b_inv = [Buf() for _ in range(4)]
        b_sm = [Buf() for _ in range(64)]
        b_eps = Buf()
        b_mxc = [[Buf() for _ in range(8)] for _ in range(2)]

        tr.op('dve', lambda e: e.memset(epsr, RMS_EPS), writes=[b_eps])
        tr.op('dve', lambda e: e.memset(epsl, LN_EPS), writes=[b_eps])
        cdeps = []
        cdeps.append(tr.dma('sp', lambda e: e.dma_start(out=pv, in_=pvec), 'cst', writes=[b_const]))
        cdeps.append(tr.dma('sp', lambda e: e.dma_start(out=cw.rearrange("p a b -> p (a b)"), in_=cwp), 'cst', writes=[b_const]))
        cdeps.append(tr.dma('sp', lambda e: e.dma_start(out=ones, in_=cst[:, 0:128]), 'cst', writes=[b_const]))
        cdeps.append(tr.dma('sp', lambda e: e.dma_start(out=idf, in_=cst[:, 128:256]), 'cst', writes=[b_const]))
        cdeps.append(tr.dma('sp', lambda e: e.dma_start(out=gf, in_=gfb), 'cst', writes=[b_gf]))
        cdeps = [cdeps[-1]]
        tr.op('act', lambda e: e.activation(out=ident, in_=idf, func=AF.Copy), reads=[b_const], writes=[b_const], extra=cdeps)

        xg = [view(65536, [D], F32), view(81920, [D], F32)]
        xs = [view(O_TMP + 16384, [D], BF16), view(98304, [D], BF16)]
        gn = view(O_TMP, [D], F32)
        hTs = [view(0, [KC, 512], BF16), view(32768, [KC, 512], BF16)]
        b_xg = [Buf(), Buf()]
        b_xs = [Buf(), Buf()]
        b_gn = Buf()
        b_hs = [[[Buf() for _ in range(8)] for _ in range(4)] for _ in range(2)]
        ss = sm[:, 0:33]
        sd = sm[:, 40:73]
        rs = sm[:, 80:113]
        b_ss = [Buf() for _ in range(NG)]
        b_sd = [Buf() for _ in range(NG)]
        b_rs = [Buf() for _ in range(NG)]
        tr.dma('sp', lambda e: e.dma_start(out=gn, in_=gnb), 'gn', writes=[b_gn])
        hst_deps = {}
        pend_hst = []

        def emit_hst(upto):
            while pend_hst and pend_hst[0][0] <= upto:
                _, stg, g0, ntok, rb = pend_hst.pop(0)
                hst_deps[stg] = tr.dma(
                    'sp', lambda e, stg=stg, g0=g0, ntok=ntok: e.dma_start(
                        out=hT_all[:, :, g0 * 128:g0 * 128 + ntok].rearrange("k p t -> p k t"),
                        in_=hTs[stg][:, :, 0:ntok]),
                    'hst%d' % stg, reads=rb)

        for g in range(NG):
            q = g % 2
            stg = (g // 4) % 2
            gl = g % 4
            tr.dma('sp', lambda e, g=g, q=q: e.dma_start(out=xg[q], in_=x_in[g * 128:(g + 1) * 128, :]),
                   'xg%d' % q, writes=[b_xg[q]])
            emit_hst(g - 2)
            tr.op('act', lambda e, g=g, q=q: e.activation(out=xs[q], in_=xg[q], func=AF.Square, accum_out=ss[:, g:g + 1]),
                  reads=[b_xg[q]], writes=[b_xs[q], b_ss[g]])
            tr.op('act', lambda e, g=g: e.activation(out=sd[:, g:g + 1], in_=ss[:, g:g + 1], func=AF.Sqrt,
                                                      scale=1.0 / D, bias=epsr),
                  reads=[b_ss[g], b_eps], writes=[b_sd[g]])
            tr.op('dve', lambda e, g=g: e.reciprocal(out=rs[:, g:g + 1], in_=sd[:, g:g + 1]),
                  reads=[b_sd[g]], writes=[b_rs[g]])
            tr.op('dve', lambda e, g=g, q=q: e.scalar_tensor_tensor(out=xs[q], in0=xg[q], scalar=rs[:, g:g + 1], in1=gn,
                                                                    op0=ALU.mult, op1=ALU.mult),
                  reads=[b_xg[q], b_rs[g], b_gn], writes=[b_xs[q]])
            for kb in range(8):
                bank = kb % 4

                def tfn(e, q=q, kb=kb, bank=bank):
                    last = None
                    for qq in range(4):
                        last = e.transpose(out=PSb(bank)[:, qq * 128:(qq + 1) * 128],
                                           in_=xs[q][:, (kb * 4 + qq) * 128:(kb * 4 + qq + 1) * 128], identity=ident)
                    return last
                tr.op('pe', tfn, reads=[b_xs[q], b_const], writes=[b_bank[bank]])
                src = PSb(bank)[:, 0:512].rearrange("p (a b) -> p a b", b=128)
                dst = hTs[stg][:, kb * 4:kb * 4 + 4, gl * 128:(gl + 1) * 128]
                if kb % 2 == 0:
                    tr.op('act', lambda e, src=src, dst=dst: e.activation(out=dst, in_=src, func=AF.Copy),
                          reads=[b_bank[bank]], writes=[b_hs[stg][gl][kb]])
                else:
                    tr.op('dve', lambda e, src=src, dst=dst: e.tensor_copy(out=dst, in_=src),
                          reads=[b_bank[bank]], writes=[b_hs[stg][gl][kb]])
            if gl == 3 or g == NG - 1:
                ntok = (gl + 1) * 128
                g0 = g - gl
                rb = [b for glb in b_hs[stg][:gl + 1] for b in glb]
                pend_hst.append((g, stg, g0, ntok, rb))
        emit_hst(NG)
        pro_done = [hst_deps[0], hst_deps[1], ('act', tr.cnt['act']), ('dve', tr.cnt['dve']), ('pe', tr.cnt['pe'])]

        wstate = {'n': 0, 'ti': 0, 'm': 0}
        wb_pending = []
        wb_dep = {}
        first_bf = {}

        def flush_wb(upto):
            while wb_pending and wb_pending[0][0] <= upto:
                n, s, dst, nelem, ckey = wb_pending.pop(0)
                wb_dep[ckey] = tr.dma('sp', lambda e, s=s, dst=dst, nelem=nelem: e.dma_start(out=dst, in_=wslot[s][:, 0:nelem]),
                                      'wb%d' % s, reads=[b_w[s]])

        def wload(src_ap, nelem, ckey, f32_ap):
            ti = wstate['ti']
            n = wstate['n']
            s = n % 4
            wstate['n'] += 1
            flush_wb(n - 3)
            if ckey not in first_bf:
                if ckey[0] == 'in' and ckey[1] // 16 in (SEG_A, SEG_B):
                    first_bf[ckey] = 1
                else:
                    first_bf[ckey] = 2 + wstate['m'] % 2
                    wstate['m'] += 1
            if ti < first_bf[ckey]:
                tr.dma('pool', lambda e, s=s: e.dma_start(out=wslot[s][:, 0:nelem], in_=f32_ap), 'wc%d' % s,
                       writes=[b_w[s]])
                if ti == first_bf[ckey] - 1:
                    wb_pending.append((n, s, src_ap, nelem, ckey))
            else:
                tr.dma('sp', lambda e, s=s: e.dma_start(out=wslot[s][:, 0:nelem], in_=src_ap), 'w%d' % s,
                       writes=[b_w[s]], extra=[wb_dep[ckey]])
            return s

        pstate = {'n': 0}

        def next_banks():
            b = (0, 1) if pstate['n'] % 2 == 0 else (2, 3)
            pstate['n'] += 1
            return b

        def mm_pair(wtiles, KH, rhs_fn, rhs_bufs, banks, N, tile0, extra=()):
            nh = len(wtiles)
            dep = None
            for h, wtile in enumerate(wtiles):
                s = wload(*wtile)
                wt = wslot[s][:, 0:KH * 256].rearrange("p (k n) -> p k n", n=256)

                def fn(e, wt=wt, h=h):
                    last = None
                    for q in range(2):
                        for kc in range(KH):
                            last = e.matmul(PS[:, banks[q], 0:N], lhsT=wt[:, kc, q * 128:(q + 1) * 128],
                                            rhs=rhs_fn(h * KH + kc), start=(h == 0 and kc == 0),
                                            stop=(h == nh - 1 and kc == KH - 1))
                    return last
                dep = tr.op('pe', fn, reads=[b_w[s]] + list(rhs_bufs), writes=[b_bank[banks[0]], b_bank[banks[1]]],
                            extra=extra)
            return dep

        def inproj_pair(seg, pr, banks, halo, tile0, extra=(), Tj=T):
            grp = seg * 16 + pr
            wt = [(wq_in[grp, :, h, :], 4096, ('in', grp, h), w_in[grp, :, h, :]) for h in range(2)]
            if halo:
                return mm_pair(wt, 16, lambda kc: hT[:, kc, 0:Tj + 32], [b_hT], banks, Tj + 32, tile0, extra)
            return mm_pair(wt, 16, lambda kc: hT[:, kc, 16:16 + Tj], [b_hT], banks, Tj, tile0, extra)

        def proj_pair(wq, w_ap, name, pr, rhs_view, rhs_bufs, banks, tile0, Tj=T):
            wt = [(wq[pr, :, h, :], 4096, (name, pr, h), w_ap[pr, :, h, :]) for h in range(2)]
            return mm_pair(wt, 16, lambda kc: rhs_view[:, kc, 0:Tj], rhs_bufs, banks, Tj, tile0)

        prev_tile_done = []
        ring = {'n': 0}
        ystore_deps = []
        deferred = []

        def flush_stores():
            nonlocal ystore_deps
            if not deferred:
                return
            ystore_deps = []
            for (tg, nt, o0) in deferred:
                ystore_deps.append(tr.dma('sp', lambda e, tg=tg, nt=nt, o0=o0: e.dma_start(out=y[o0:o0 + nt, :], in_=ybuf[tg][0:nt, :]),
                                          'ys%d' % tg, reads=[b_y[tg]]))
            del deferred[:]

        for ti in range(ntiles):
            c, j = divmod(ti, NTC)
            base = c * CHR + j * T
            tile0 = (ti == 0)
            wstate['ti'] = ti
            Tj = 384 if j == NTC - 1 else T
            Wj = Tj + 32
            tx = list(pro_done) if tile0 else list(prev_tile_done)
            txg = list(pro_done) if tile0 else []

            def load_hT(dq, base, extra, Wj_):
                tr.dma(dq, lambda e, base=base, Wj_=Wj_, Tj=Tj: e.dma_start(out=hT[:, :, 0:Wj_], in_=hT_all[:, :, base:base + Wj_].rearrange("k p t -> p k t")),
                       'hT', writes=[b_hT], extra=extra)
                tr.dma(dq, lambda e, base=base, Wj_=Wj_, Tj=Tj: e.dma_start(out=mk[:, 0:Wj_], in_=maskb[:, base:base + Wj_]), 'mk', writes=[b_mk])
            if tile0:
                load_hT('sp', base, pro_done, Wj)

            m2, m4, m8, m16 = (tmpv[0][:, 0:Wj - 1], tmpv[1][:, 0:Wj - 3], tmpv[2][:, 0:Wj - 7], tmpv[3][:, 0:Wj - 15])
            tr.op('dve', lambda e, m2=m2, m4=m4, m8=m8, m16=m16, Tj=Tj, Wj=Wj: e.tensor_tensor(out=m2, in0=mk[:, 0:Wj - 1], in1=mk[:, 1:Wj], op=ALU.add),
                  reads=[b_mk], writes=[b_t[0]], extra=txg)
            tr.op('dve', lambda e, m2=m2, m4=m4, m8=m8, m16=m16, Tj=Tj, Wj=Wj: e.tensor_tensor(out=m4, in0=m2[:, 0:Wj - 3], in1=m2[:, 2:Wj - 1], op=ALU.add),
                  reads=[b_t[0]], writes=[b_t[1]], extra=txg)
            tr.op('dve', lambda e, m2=m2, m4=m4, m8=m8, m16=m16, Tj=Tj, Wj=Wj: e.tensor_tensor(out=m8, in0=m4[:, 0:Wj - 7], in1=m4[:, 4:Wj - 3], op=ALU.add),
                  reads=[b_t[1]], writes=[b_t[2]], extra=txg)
            tr.op('dve', lambda e, m2=m2, m4=m4, m8=m8, m16=m16, Tj=Tj, Wj=Wj: e.tensor_tensor(out=m16, in0=m8[:, 0:Wj - 15], in1=m8[:, 8:Wj - 7], op=ALU.add),
                  reads=[b_t[2]], writes=[b_t[3]], extra=txg)
            for gi, (srcm, off) in enumerate(((m2, 15), (m4, 14), (m8, 12), (m16, 8))):
                tr.op('dve', lambda e, gi=gi, srcm=srcm, off=off, Tj=Tj, Wj=Wj: e.tensor_scalar(
                    out=inv[gi][:, 0:Tj], in0=srcm[:, off:off + Tj], scalar1=1.0, scalar2=None, op0=ALU.max),
                    reads=[b_t[gi]], writes=[b_inv[gi]])
                tr.op('dve', lambda e, gi=gi, Tj=Tj, Wj=Wj: e.reciprocal(out=inv[gi][:, 0:Tj], in_=inv[gi][:, 0:Tj]), reads=[b_inv[gi]], writes=[b_inv[gi]])

            pending_stats = []

            def flush_stats(upto):
                while pending_stats and pending_stats[0] <= upto:
                    i = pending_stats.pop(0)
                    qq = i % 2
                    tr.op('pe', lambda e, i=i, Tj=Tj, Wj=Wj: e.matmul(PS[:, 6, 0:Tj], lhsT=ones, rhs=cbuf[:, i, 0:Tj], start=(i == 0), stop=(i == KC - 1)),
                          reads=[b_c[i], b_const], writes=[b_bank[6]])
                    tr.op('pe', lambda e, i=i, qq=qq, Tj=Tj, Wj=Wj: e.matmul(PS[:, 7, 0:Tj], lhsT=ones, rhs=tmpv[6 + qq][:, 0:Tj], start=(i == 0), stop=(i == KC - 1)),
                          reads=[b_t[6 + qq], b_const], writes=[b_bank[7]])

            def emit_sq(ip):
                for q in range(2):
                    i = 2 * ip + q
                    tr.op('act', lambda e, i=i, q=q, Tj=Tj, Wj=Wj: e.activation(out=tmpv[6 + q][:, 0:Tj], in_=cbuf[:, i, 0:Tj], func=AF.Square),
                          reads=[b_c[i]], writes=[b_t[6 + q]])
                    pending_stats.append(i)

            for ip in range(16):
                bb = next_banks()
                inproj_pair(SEG_B, ip, bb, True, tile0, extra=txg, Tj=Tj)
                for q in range(2):
                    tr.op('act', lambda e, q=q, bb=bb, Tj=Tj, Wj=Wj: e.activation(out=tmpv[q][:, 0:Wj], in_=PS[:, bb[q], 0:Wj], func=AF.Sigmoid),
                          reads=[b_bank[bb[q]]], writes=[b_t[q]], extra=txg)
                if ip > 0:
                    emit_sq(ip - 1)
                ab = next_banks()
                inproj_pair(SEG_A, ip, ab, True, tile0, extra=txg, Tj=Tj)
                if ip == 0:
                    flush_stores()
                    tx = tx + ystore_deps
                flush_stats(2 * ip - 1)
                vs = [2 + (2 * ip + q) % 4 for q in range(2)]
                for q in range(2):
                    tr.op('dve', lambda e, q=q, ab=ab, vs=vs, Tj=Tj, Wj=Wj: e.tensor_tensor(out=tmpv[vs[q]][:, 0:Wj], in0=PS[:, ab[q], 0:Wj], in1=tmpv[q][:, 0:Wj], op=ALU.mult),
                          reads=[b_bank[ab[q]], b_t[q]], writes=[b_t[vs[q]]], extra=txg)
                npool = 0 if ti < 1 else NPOOL
                kd = 31 - npool
                if npool:
                    for k in range(kd, 31):
                        for q in range(2):
                            i = 2 * ip + q
                            vin = tmpv[vs[q]][:, k + 1:k + 1 + Tj]
                            ppv = tmpv[12 + q][:, 0:Tj]
                            if k == kd:
                                tr.op('act', lambda e, ppv=ppv, vin=vin, i=i, k=k: e.activation(
                                    out=ppv, in_=vin, func=AF.Identity, scale=cw[:, i, k:k + 1]),
                                    reads=[b_t[vs[q]], b_const], writes=[b_t[12 + q]], extra=tx)
                            else:
                                r = 14 + ring['n'] % 3
                                ring['n'] += 1
                                ptv = tmpv[r][:, 0:Tj]
                                tr.op('act', lambda e, ptv=ptv, vin=vin, i=i, k=k: e.activation(
                                    out=ptv, in_=vin, func=AF.Identity, scale=cw[:, i, k:k + 1]),
                                    reads=[b_t[vs[q]], b_const], writes=[b_t[r]], extra=tx)
                                tr.op('pool', lambda e, ppv=ppv, ptv=ptv: e.tensor_tensor(out=ppv, in0=ppv, in1=ptv, op=ALU.add),
                                      reads=[b_t[12 + q], b_t[r]], writes=[b_t[12 + q]])
                for k in range(kd):
                    for q in range(2):
                        i = 2 * ip + q
                        acc = PS[:, 4 + q, 0:Tj]
                        vin = tmpv[vs[q]][:, k + 1:k + 1 + Tj]
                        if k == 0:
                            tr.op('dve', lambda e, acc=acc, vin=vin, i=i, Tj=Tj, Wj=Wj: e.tensor_scalar(
                                out=acc, in0=vin, scalar1=cw[:, i, 0:1], scalar2=pcol(PV_CB, i), op0=ALU.mult, op1=ALU.add),
                                reads=[b_t[vs[q]], b_const], writes=[b_bank[4 + q]], extra=txg)
                        elif k < 30:
                            tr.op('dve', lambda e, acc=acc, vin=vin, i=i, k=k, Tj=Tj, Wj=Wj: e.scalar_tensor_tensor(
                                out=acc, in0=vin, scalar=cw[:, i, k:k + 1], in1=acc, op0=ALU.mult, op1=ALU.add),
                                reads=[b_t[vs[q]], b_bank[4 + q]], writes=[b_bank[4 + q]])
                        else:
                            tr.op('dve', lambda e, acc=acc, vin=vin, i=i, k=k, Tj=Tj, Wj=Wj: e.scalar_tensor_tensor(
                                out=cbuf[:, i, 0:Tj], in0=vin, scalar=cw[:, i, k:k + 1], in1=acc, op0=ALU.mult, op1=ALU.add),
                                reads=[b_t[vs[q]], b_bank[4 + q]], writes=[b_c[i]], extra=tx)
                if npool:
                    for q in range(2):
                        i = 2 * ip + q
                        tr.op('dve', lambda e, q=q, i=i, Tj=Tj, Wj=Wj: e.tensor_tensor(out=cbuf[:, i, 0:Tj], in0=PS[:, 4 + q, 0:Tj], in1=tmpv[12 + q][:, 0:Tj], op=ALU.add),
                              reads=[b_bank[4 + q], b_t[12 + q]], writes=[b_c[i]], extra=tx)

            def emit_z(ip):
                zb = next_banks()
                inproj_pair(SEG_ZC, ip, zb, False, tile0, Tj=Tj)
                for q in range(2):
                    szs = 8 + 2 * (ip % 2) + q
                    tr.op('act', lambda e, q=q, zb=zb, szs=szs, Tj=Tj, Wj=Wj: e.activation(out=tmpv[szs][:, 0:Tj], in_=PS[:, zb[q], 0:Tj], func=AF.Silu),
                          reads=[b_bank[zb[q]]], writes=[b_t[szs]], extra=tx)
            emit_z(0)
            emit_z(1)
            emit_sq(15)
            flush_stats(KC)

            tA, tB = tmpv[0][:, 0:Tj], tmpv[1][:, 0:Tj]
            tr.op('dve', lambda e, tA=tA, tB=tB, Tj=Tj, Wj=Wj: e.tensor_scalar(out=tA, in0=PS[:, 6, 0:Tj], scalar1=1.0 / D, scalar2=None, op0=ALU.mult),
                  reads=[b_bank[6]], writes=[b_t[0]])
            tr.op('dve', lambda e, tA=tA, tB=tB, Tj=Tj, Wj=Wj: e.tensor_tensor(out=tB, in0=tA, in1=tA, op=ALU.mult), reads=[b_t[0]], writes=[b_t[1]])
            tr.op('dve', lambda e, tA=tA, tB=tB, Tj=Tj, Wj=Wj: e.scalar_tensor_tensor(out=tB, in0=PS[:, 7, 0:Tj], scalar=1.0 / D, in1=tB, op0=ALU.mult, op1=ALU.subtract),
                  reads=[b_bank[7], b_t[1]], writes=[b_t[1]])
            tr.op('act', lambda e, tA=tA, tB=tB, Tj=Tj, Wj=Wj: e.activation(out=tB, in_=tB, func=AF.Sqrt, bias=epsl, scale=1.0), reads=[b_t[1], b_eps], writes=[b_t[1]])
            tr.op('dve', lambda e, tA=tA, tB=tB, Tj=Tj, Wj=Wj: e.reciprocal(out=PS[:, 6, 0:Tj], in_=tB), reads=[b_t[1]], writes=[b_bank[6]])
            tr.op('dve', lambda e, tA=tA, tB=tB, Tj=Tj, Wj=Wj: e.scalar_tensor_tensor(out=PS[:, 7, 0:Tj], in0=tA, scalar=-1.0, in1=PS[:, 6, 0:Tj], op0=ALU.mult, op1=ALU.mult),
                  reads=[b_t[0], b_bank[6]], writes=[b_bank[7]])

            for ip in range(16):
                if ip >= 2:
                    emit_z(ip)
                for q in range(2):
                    i = 2 * ip + q
                    tr.op('dve', lambda e, q=q, i=i, Tj=Tj, Wj=Wj: e.tensor_tensor(out=tmpv[4 + q][:, 0:Tj], in0=cbuf[:, i, 0:Tj], in1=PS[:, 6, 0:Tj], op=ALU.mult),
                          reads=[b_c[i], b_bank[6]], writes=[b_t[4 + q]])
                for q in range(2):
                    tr.op('dve', lambda e, q=q, Tj=Tj, Wj=Wj: e.tensor_tensor(out=tmpv[4 + q][:, 0:Tj], in0=tmpv[4 + q][:, 0:Tj], in1=PS[:, 7, 0:Tj], op=ALU.add),
                          reads=[b_t[4 + q], b_bank[7]], writes=[b_t[4 + q]])
                for q in range(2):
                    i = 2 * ip + q
                    tr.op('act', lambda e, q=q, i=i, Tj=Tj, Wj=Wj: e.activation(out=tmpv[6 + q][:, 0:Tj], in_=tmpv[4 + q][:, 0:Tj], func=AF.Silu,
                                                                  scale=pcol(PV_LNG, i), bias=pcol(PV_LNB, i)),
                          reads=[b_t[4 + q], b_const], writes=[b_t[6 + q]])
                for q in range(2):
                    i = 2 * ip + q
                    szs = 8 + 2 * (ip % 2) + q
                    tr.op('dve', lambda e, q=q, i=i, szs=szs, Tj=Tj, Wj=Wj: e.tensor_tensor(out=yc_in[:, i, 0:Tj], in0=tmpv[6 + q][:, 0:Tj], in1=tmpv[szs][:, 0:Tj], op=ALU.mult),
                          reads=[b_t[6 + q], b_t[szs]], writes=[b_yc[i]], extra=tx)
            c_done = [('dve', tr.cnt['dve']), ('pe', tr.cnt['pe']), ('act', tr.cnt['act'])]

            mx = [view(O_TMP + 9 * 1792, [8, T], BF16), view(O_TMP + 13 * 1792, [8, T], BF16)]
            for g in range(4):
                kwin = (2, 4, 8, 16)[g]
                mxi = g % 2
                for up in range(4):
                    ub = next_banks()
                    inproj_pair(SEG_U, g * 4 + up, ub, True, tile0, Tj=Tj)
                    for q in range(2):
                        tr.op('act', lambda e, q=q, ub=ub, Tj=Tj, Wj=Wj: e.activation(out=tmpv[q][:, 0:Wj], in_=PS[:, ub[q], 0:Wj], func=AF.Copy),
                              reads=[b_bank[ub[q]]], writes=[b_t[q]])
                    for q in range(2):
                        tr.op('dve', lambda e, q=q, ub=ub, Tj=Tj, Wj=Wj: e.tensor_tensor(out=tmpv[2 + q][:, 0:Wj - 1], in0=PS[:, ub[q], 0:Wj - 1], in1=tmpv[q][:, 1:Wj], op=ALU.add),
                              reads=[b_bank[ub[q]], b_t[q]], writes=[b_t[2 + q]])
                    cur = [2, 3]
                    off = 15
                    ln = Wj - 1
                    for step, sh in ((4, 2), (8, 4), (16, 8)):
                        if kwin < step:
                            break
                        nxt = [4, 5] if cur[0] == 2 else [2, 3]
                        nl = ln - sh
                        for q in range(2):
                            tr.op('dve', lambda e, q=q, cur=cur, nxt=nxt, nl=nl, sh=sh, Tj=Tj, Wj=Wj: e.tensor_tensor(
                                out=tmpv[nxt[q]][:, 0:nl], in0=tmpv[cur[q]][:, 0:nl], in1=tmpv[cur[q]][:, sh:sh + nl], op=ALU.add),
                                reads=[b_t[cur[q]]], writes=[b_t[nxt[q]]])
                        cur = nxt
                        ln = nl
                        off -= sh // 2
                    for q in range(2):
                        tr.op('dve', lambda e, q=q, cur=cur, off=off, g=g, Tj=Tj, Wj=Wj: e.tensor_tensor(
                            out=tmpv[6 + q][:, 0:Tj], in0=tmpv[cur[q]][:, off:off + Tj], in1=inv[g][:, 0:Tj], op=ALU.mult),
                            reads=[b_t[cur[q]], b_inv[g]], writes=[b_t[6 + q]])
                    for q in range(2):
                        tr.op('dve', lambda e, q=q, up=up, mxi=mxi, Tj=Tj, Wj=Wj: e.tensor_tensor(
                            out=mx[mxi][:, 2 * up + q, 0:Tj], in0=tmpv[6 + q][:, 0:Tj], in1=tmpv[q][:, 16:16 + Tj], op=ALU.subtract),
                            reads=[b_t[6 + q], b_t[q]], writes=[b_mxc[mxi][2 * up + q]], extra=tx)
                for opi in range(4):
                    zb = next_banks()
                    inproj_pair(SEG_ZP, g * 4 + opi, zb, False, tile0, Tj=Tj)
                    for q in range(2):
                        tr.op('act', lambda e, q=q, zb=zb, Tj=Tj, Wj=Wj: e.activation(out=tmpv[7 - q][:, 0:Tj], in_=PS[:, zb[q], 0:Tj], func=AF.Silu),
                              reads=[b_bank[zb[q]]], writes=[b_t[7 - q]])
                    wb = next_banks()
                    mm_pair([(wq_pool[g, opi], 2048, ('pool', g, opi), w_pool[g, opi])], 8,
                            lambda kc, mxi=mxi, Tj=Tj: mx[mxi][:, kc, 0:Tj], b_mxc[mxi], wb, Tj, tile0)
                    for q in range(2):
                        i = g * 8 + 2 * opi + q
                        tr.op('dve', lambda e, q=q, i=i, wb=wb, Tj=Tj, Wj=Wj: e.scalar_tensor_tensor(
                            out=yp_in[:, i, 0:Tj], in0=PS[:, wb[q], 0:Tj], scalar=pcol(PV_LS, i), in1=tmpv[7 - q][:, 0:Tj], op0=ALU.mult, op1=ALU.mult),
                            reads=[b_bank[wb[q]], b_t[7 - q], b_const], writes=[b_yp[i]], extra=c_done + tx)

            for dp in range(16):
                gb = next_banks()
                inproj_pair(SEG_GP, dp, gb, False, tile0, Tj=Tj)
                for q in range(2):
                    tr.op('act', lambda e, q=q, gb=gb, Tj=Tj, Wj=Wj: e.activation(out=tmpv[q][:, 0:Tj], in_=PS[:, gb[q], 0:Tj], func=AF.Sigmoid),
                          reads=[b_bank[gb[q]]], writes=[b_t[q]])
                gcb = next_banks()
                inproj_pair(SEG_GC, dp, gcb, False, tile0, Tj=Tj)
                for q in range(2):
                    tr.op('act', lambda e, q=q, gcb=gcb, Tj=Tj, Wj=Wj: e.activation(out=tmpv[2 + q][:, 0:Tj], in_=PS[:, gcb[q], 0:Tj], func=AF.Sigmoid),
                          reads=[b_bank[gcb[q]]], writes=[b_t[2 + q]])
                pb = next_banks()
                proj_pair(wq_pp, w_pp, 'pp', dp, yp_in, b_yp, pb, tile0, Tj=Tj)
                for q in range(2):
                    tr.op('dve', lambda e, q=q, pb=pb, Tj=Tj, Wj=Wj: e.tensor_tensor(out=tmpv[4 + q][:, 0:Tj], in0=PS[:, pb[q], 0:Tj], in1=tmpv[q][:, 0:Tj], op=ALU.mult),
                          reads=[b_bank[pb[q]], b_t[q]], writes=[b_t[4 + q]])
                cb_ = next_banks()
                proj_pair(wq_pc, w_pc, 'pc', dp, yc_in, b_yc, cb_, tile0, Tj=Tj)
                for q in range(2):
                    i = 2 * dp + q
                    tr.op('dve', lambda e, q=q, i=i, cb_=cb_, Tj=Tj, Wj=Wj: e.scalar_tensor_tensor(
                        out=tmpv[6 + q][:, 0:Tj], in0=PS[:, cb_[q], 0:Tj], scalar=pcol(PV_BPC, i), in1=tmpv[2 + q][:, 0:Tj], op0=ALU.add, op1=ALU.mult),
                        reads=[b_bank[cb_[q]], b_t[2 + q], b_const], writes=[b_t[6 + q]])
                for q in range(2):
                    i = 2 * dp + q
                    tr.op('dve', lambda e, q=q, i=i, Tj=Tj, Wj=Wj: e.tensor_tensor(out=merged[:, i, 0:Tj], in0=tmpv[4 + q][:, 0:Tj], in1=tmpv[6 + q][:, 0:Tj], op=ALU.add),
                          reads=[b_t[4 + q], b_t[6 + q]], writes=[b_mg[i]], extra=c_done)
            p4_done = [('pe', tr.cnt['pe']), ('dve', tr.cnt['dve']), ('act', tr.cnt['act'])]

            if ti + 1 < ntiles:
                c2, j2 = divmod(ti + 1, NTC)
                load_hT('sp', c2 * CHR + j2 * T, [], (384 if j2 == NTC - 1 else T) + 32)

            ntoks = [128, 128, 128, 32]
            ngrp = 3 if j == NTC - 1 else 4
            ssq = [sm[:, 128 + tg * 8:128 + tg * 8 + 8] for tg in range(4)]
            for dblk in range(8):
                bset = (dblk % 2) * 4
                for kq in range(4):
                    s = wload(wq_out[dblk, :, kq, :], 4096, ('out', dblk, kq), w_out[dblk, :, kq, :])
                    wt = wslot[s].rearrange("p (k n) -> p k n", n=512)

                    def fn(e, wt=wt, kq=kq, bset=bset, ngrp=ngrp):
                        last = None
                        for tg in range(ngrp):
                            nt = ntoks[tg]
                            for kc in range(8):
                                last = e.matmul(PS[0:nt, bset + tg, :], lhsT=merged[:, kq * 8 + kc, tg * 128:tg * 128 + nt],
                                                rhs=wt[:, kc, :], start=(kq == 0 and kc == 0), stop=(kq == 3 and kc == 7))
                        return last
                    tr.op('pe', fn, reads=[b_w[s]] + b_mg, writes=[b_bank[bset + tg] for tg in range(ngrp)])
                if dblk == 0:
                  for tg in range(ngrp):
                    r0 = base + 16 + tg * 128
                    nt = ntoks[tg]
                    tr.dma('sp', lambda e, tg=tg, r0=r0, nt=nt, Tj=Tj, Wj=Wj: e.dma_start(out=ybuf[tg][0:nt, :], in_=x_in[r0:r0 + nt, :]),
                           'yl%d' % tg, writes=[b_y[tg]], extra=p4_done + ystore_deps)
                for tg in range(ngrp):
                    nt = ntoks[tg]
                    ysl = ybuf[tg][0:nt, dblk * 512:(dblk + 1) * 512]
                    tr.op('dve', lambda e, tg=tg, nt=nt, ysl=ysl, bset=bset, Tj=Tj, Wj=Wj: e.tensor_tensor(out=ysl, in0=PS[0:nt, bset + tg, :], in1=ysl, op=ALU.add),
                          reads=[b_bank[bset + tg], b_y[tg]], writes=[b_y[tg]])
                    tr.op('act', lambda e, tg=tg, nt=nt, ysl=ysl, dblk=dblk, Tj=Tj, Wj=Wj: e.activation(
                        out=junkv[tg % 2][0:nt, :], in_=ysl, func=AF.Square, accum_out=ssq[tg][0:nt, dblk:dblk + 1]),
                        reads=[b_y[tg]], writes=[b_t[4 + 2 * (tg % 2)], b_t[5 + 2 * (tg % 2)], b_sm[tg * 8 + dblk]], extra=p4_done)
            for tg in range(ngrp):
                nt = ntoks[tg]
                tot = sm[0:nt, 192 + tg:193 + tg]
                tr.op('dve', lambda e, tg=tg, nt=nt, tot=tot, Tj=Tj, Wj=Wj: e.tensor_reduce(out=tot, in_=ssq[tg][0:nt, :], axis=mybir.AxisListType.X, op=ALU.add),
                      reads=[b_sm[tg * 8 + d_] for d_ in range(8)], writes=[b_sm[40 + tg]])
                tr.op('act', lambda e, nt=nt, tot=tot, Tj=Tj, Wj=Wj: e.activation(out=tot, in_=tot, func=AF.Sqrt, scale=1.0 / D, bias=epsr[0:nt, :]),
                      reads=[b_sm[40 + tg], b_eps], writes=[b_sm[40 + tg]])
                tr.op('dve', lambda e, tot=tot, Tj=Tj, Wj=Wj: e.reciprocal(out=tot, in_=tot), reads=[b_sm[40 + tg]], writes=[b_sm[40 + tg]])
                tr.op('dve', lambda e, tg=tg, nt=nt, tot=tot, Tj=Tj, Wj=Wj: e.scalar_tensor_tensor(
                    out=ybuf[tg][0:nt, :], in0=ybuf[tg][0:nt, :], scalar=tot, in1=gf[0:nt, :], op0=ALU.mult, op1=ALU.mult),
                    reads=[b_y[tg], b_sm[40 + tg], b_gf], writes=[b_y[tg]])
                o0 = c * 2048 + j * T + tg * 128
                deferred.append((tg, nt, o0))
            prev_tile_done = [('pe', tr.cnt['pe']), ('dve', tr.cnt['dve']), ('act', tr.cnt['act'])]
            flush_wb(1 << 60)

        flush_stores()
        tr.wait('sp', ystore_deps + [(k, v) for k, v in tr.dcnt.items() if k.startswith('ys') or k.startswith('wb')])

        for k in list(tr.dcnt.keys()):
            sem(k)
        with nc.Block() as block:
            def run(eng_obj, key):
                for o in tr.ops[key]:
                    if o[0] == 'w':
                        eng_obj.wait_ge(sem(o[1]), o[2])
                    elif o[0] == 'o':
                        o[1](eng_obj).then_inc(sem(key), 1)
                    else:
                        o[1](eng_obj).then_inc(sem(o[2]), 16)

            @block.sync
            def _(e):
                run(e, 'sp')

            @block.gpsimd
            def _(e):
                run(e, 'pool')

            @block.tensor
            def _(e):
                run(e, 'pe')

            @block.scalar
            def _(e):
                run(e, 'act')

            @block.vector
            def _(e):
                run(e, 'dve')
    return nc


def _prep_core(i, xp, xsm):
    xin = np.zeros((ROWS, D), np.float32)
    mask = np.zeros((ROWS,), np.float32)
    for c in range(NCH):
        if c == 0:
            seq, s0 = xp[i], 0
        else:
            seq, s0 = xsm[i // 2], (i % 2) * 2048
        L = seq.shape[0]
        lo = s0 - 16
        hi = s0 + CHR - 16
        a, b = max(lo, 0), min(hi, L)
        xin[c * CHR + (a - lo): c * CHR + (b - lo)] = seq[a:b]
        mask[c * CHR + (a - lo): c * CHR + (b - lo)] = 1.0
    return xin, np.ascontiguousarray(np.broadcast_to(mask[None, :], (128, ROWS)))


_NC_CACHE = {}


def _prep_shared(g_norm, w_in, w_pool, ls_pool, w_proj_pool, conv_w, conv_b, ln_g, ln_b, w_proj_conv, b_proj_conv,
                 w_out, g_final):
    f = lambda a: np.ascontiguousarray(np.asarray(a, dtype=np.float32))
    pvec = np.zeros((128, 160), np.float32)
    for jv, vec in enumerate((ls_pool, conv_b, ln_g, ln_b, b_proj_conv)):
        pvec[:, jv * 32:(jv + 1) * 32] = f(vec).reshape(32, 128).T
    cwp = np.ascontiguousarray(f(conv_w).reshape(31, 32, 128).transpose(2, 1, 0).reshape(128, 992))
    cst = np.concatenate([np.ones((128, 128), np.float32), np.eye(128, dtype=np.float32)], axis=1)
    blk = lambda w, ng: np.ascontiguousarray(w.reshape(2, 16, 128, ng, 256).transpose(3, 2, 0, 1, 4)).reshape(ng, 128, 2, 4096)
    return {
        "w_in": blk(f(w_in).reshape(D, NIN), 112),
        "w_pool": np.ascontiguousarray(f(w_pool).reshape(4, 8, 128, 4, 256).transpose(0, 3, 2, 1, 4)).reshape(4, 4, 128, 2048),
        "w_pp": blk(f(w_proj_pool).reshape(D, D), 16), "w_pc": blk(f(w_proj_conv).reshape(D, D), 16),
        "w_out": np.ascontiguousarray(f(w_out).reshape(4, 8, 128, 8, 512).transpose(3, 2, 0, 1, 4)).reshape(8, 128, 4, 4096),
        "gnb": np.ascontiguousarray(np.broadcast_to(f(g_norm).reshape(1, D), (128, D))),
        "gfb": np.ascontiguousarray(np.broadcast_to(f(g_final).reshape(1, D), (128, D))),
        "pvec": pvec, "cwp": cwp, "cst": cst,
    }


def kernel(x_prompt, x_sample, g_norm, w_in, w_pool, ls_pool, w_proj_pool, conv_w, conv_b, ln_g, ln_b,
           w_proj_conv, b_proj_conv, w_out, g_final):
    f = lambda a: np.ascontiguousarray(np.asarray(a, dtype=np.float32))
    xp, xsm = f(x_prompt), f(x_sample)
    shared = _prep_shared(g_norm, w_in, w_pool, ls_pool, w_proj_pool, conv_w, conv_b, ln_g, ln_b, w_proj_conv,
                          b_proj_conv, w_out, g_final)
    in_maps = []
    for i in range(8):
        xin, mb = _prep_core(i, xp, xsm)
        m = dict(shared)
        m["x_in"] = xin
        m["maskb"] = mb
        in_maps.append(m)
    if "nc" not in _NC_CACHE:
        _NC_CACHE["nc"] = build_nc()
    res = run_bass_kernel_spmd(_NC_CACHE["nc"], in_maps, core_ids=list(range(8)))
    y_prompt = np.empty((8, 2048, D), np.float32)
    y_sample = np.empty((4, 4096, D), np.float32)
    for i in range(8):
        yo = np.asarray(res.results[i]["y"], dtype=np.float32).reshape(NCH * 2048, D)
        y_prompt[i] = yo[0:2048]
        s0 = (i % 2) * 2048
        y_sample[i // 2, s0:s0 + 2048] = yo[2048:4096]
    return (y_prompt, y_sample)
```

```python
import contextlib
import numpy as np
import concourse.bass as bass
import concourse.mybir as mybir
from concourse.bass_utils import run_bass_kernel_spmd
from concourse.alu_op_type import AluOpType as ALU

F32 = mybir.dt.float32
BF16 = mybir.dt.bfloat16
F32R = mybir.dt.float32r
AF = mybir.ActivationFunctionType

D = 4096
KC = 32
T = 416
WIN = 448
NTC = 5
CHR = 2112
NCH = 2
ROWS = NCH * CHR
NG = ROWS // 128
NIN = 28672
SEG_U, SEG_ZP, SEG_A, SEG_B, SEG_ZC, SEG_GP, SEG_GC = range(7)
RMS_EPS = 1e-6
NCONV = 168
NPOOL = 12
LN_EPS = 1e-5
ENG = ('pe', 'act', 'dve', 'sp', 'pool')

O_RH = 0
O_BIG = O_RH + 28672
O_RA = O_BIG
O_RB = O_BIG + 53248
O_GF = O_BIG + 79872
O_TMP = O_GF + 16384
NTMP = 17
O_WR = O_TMP + NTMP * 1792 + 256
O_ONES = O_WR + 32768
O_IDENT = O_ONES + 512
O_IDF = O_IDENT + 256
O_PV = O_IDF + 512
O_CW = O_PV + 640
O_MK = O_CW + 3968
O_INV = O_MK + 1792
O_SM = O_INV + 4 * 1664
O_END = O_SM + 1024


class Buf:
    __slots__ = ('w', 'r')

    def __init__(self):
        self.w = None
        self.r = {}


class Tr:
    def __init__(self):
        self.ops = {e: [] for e in ENG}
        self.cnt = {e: 0 for e in ENG}
        self.waited = {e: {} for e in ENG}
        self.dcnt = {}

    def wait(self, eng, deps):
        best = {}
        for d in deps:
            if d is None:
                continue
            k, v = d
            if v > best.get(k, 0):
                best[k] = v
        for k, v in best.items():
            if k == eng and eng == 'pe':
                continue
            if self.waited[eng].get(k, 0) >= v:
                continue
            self.waited[eng][k] = v
            self.ops[eng].append(('w', k, v))

    def _deps(self, reads, writes, extra):
        deps = list(extra)
        for b in reads:
            deps.append(b.w)
        for b in writes:
            deps.append(b.w)
            deps.extend(b.r.items())
        return deps

    def _upd(self, me, reads, writes):
        for b in reads:
            if b.r.get(me[0], 0) < me[1]:
                b.r[me[0]] = me[1]
        for b in writes:
            b.w = me
            b.r = {}

    def op(self, eng, fn, reads=(), writes=(), extra=()):
        self.wait(eng, self._deps(reads, writes, extra))
        self.cnt[eng] += 1
        me = (eng, self.cnt[eng])
        self.ops[eng].append(('o', fn))
        self._upd(me, reads, writes)
        return me

    def dma(self, eng, fn, semkey, reads=(), writes=(), extra=()):
        self.wait(eng, self._deps(reads, writes, extra))
        self.dcnt[semkey] = self.dcnt.get(semkey, 0) + 16
        me = (semkey, self.dcnt[semkey])
        self.ops[eng].append(('d', fn, semkey))
        self._upd(me, reads, writes)
        return me


def build_nc(ntiles=NCH * NTC):
    nc = bass.Bass("TRN2", target_bir_lowering=False)
    dt = nc.dram_tensor
    x_in = dt("x_in", [ROWS, D], F32, kind="ExternalInput").ap()
    maskb = dt("maskb", [128, ROWS], F32, kind="ExternalInput").ap()
    w_in = dt("w_in", [112, 128, 2, 4096], F32, kind="ExternalInput").ap()
    w_pool = dt("w_pool", [4, 4, 128, 2048], F32, kind="ExternalInput").ap()
    w_pp = dt("w_pp", [16, 128, 2, 4096], F32, kind="ExternalInput").ap()
    w_pc = dt("w_pc", [16, 128, 2, 4096], F32, kind="ExternalInput").ap()
    w_out = dt("w_out", [8, 128, 4, 4096], F32, kind="ExternalInput").ap()
    gnb = dt("gnb", [128, D], F32, kind="ExternalInput").ap()
    gfb = dt("gfb", [128, D], F32, kind="ExternalInput").ap()
    pvec = dt("pvec", [128, 160], F32, kind="ExternalInput").ap()
    cwp = dt("cwp", [128, 992], F32, kind="ExternalInput").ap()
    cst = dt("cst", [128, 256], F32, kind="ExternalInput").ap()
    y = dt("y", [NCH * 2048, D], F32, kind="ExternalOutput").ap()
    hT_all = dt("hT_all", [KC, 128, ROWS], BF16, kind="Internal").ap()
    wq_in = dt("wq_in", [112, 128, 2, 4096], BF16, kind="Internal").ap()
    wq_pool = dt("wq_pool", [4, 4, 128, 2048], BF16, kind="Internal").ap()
    wq_pp = dt("wq_pp", [16, 128, 2, 4096], BF16, kind="Internal").ap()
    wq_pc = dt("wq_pc", [16, 128, 2, 4096], BF16, kind="Internal").ap()
    wq_out = dt("wq_out", [8, 128, 4, 4096], BF16, kind="Internal").ap()

    tr = Tr()

    with contextlib.ExitStack() as es:
        S = es.enter_context(nc.sbuf_tensor("S", [128, O_END // 4], F32))
        PS = es.enter_context(nc.psum_tensor("PS", [128, 8, 512], F32))
        sems = {}

        def sem(key):
            if key not in sems:
                sems[key] = es.enter_context(nc.semaphore("s_" + key))
            return sems[key]

        for e in ('pe', 'act', 'dve', 'pool'):
            sem(e)

        def view(off, shape, dtype):
            n = int(np.prod(shape))
            esz = 4 if dtype == F32 else 2
            ap = S[:, off // 4: off // 4 + (n * esz + 3) // 4]
            if dtype != F32:
                ap = ap.bitcast(dtype)
            if len(shape) == 2:
                ap = ap.rearrange("p (a b) -> p a b", b=shape[1])
            return ap

        def PSb(bank):
            return PS[:, bank, :].bitcast(BF16)

        hT = view(O_RH, [KC, WIN], BF16)
        cbuf = view(O_RA, [KC, T], F32)
        merged = view(O_RA, [KC, T], BF16)
        yp_in = view(O_RA + 26624, [KC, T], BF16)
        yc_in = view(O_RB, [KC, T], BF16)
        ybuf = [view(O_RA + 26624 + i * 16384, [D], F32) for i in range(3)] + [view(O_TMP + 8 * 1792, [D], F32)]
        gf = view(O_GF, [D], F32)
        tmpv = [view(O_TMP + i * 1792, [WIN], F32) for i in range(NTMP)]
        wslot = [view(O_WR + i * 8192, [4096], BF16) for i in range(4)]
        ones = view(O_ONES, [128], F32)
        ident = view(O_IDENT, [128], BF16)
        idf = view(O_IDF, [128], F32)
        pv = view(O_PV, [160], F32)
        cw = view(O_CW, [KC, 31], F32)
        mk = view(O_MK, [WIN], F32)
        inv = [view(O_INV + i * 1664, [T], F32) for i in range(4)]
        sm = view(O_SM, [256], F32)
        epsr = sm[:, 200:201]
        epsl = sm[:, 201:202]
        junkv = [view(O_TMP + 4 * 1792, [512], F32), view(O_TMP + 6 * 1792, [512], F32)]

        def pcol(j, i):
            return pv[:, j * 32 + i: j * 32 + i + 1]
        PV_LS, PV_CB, PV_LNG, PV_LNB, PV_BPC = range(5)

        b_hT = Buf()
        b_c = [Buf() for _ in range(KC)]
        b_yc = [Buf() for _ in range(KC)]
        b_yp = [Buf() for _ in range(KC)]
        b_mg = [Buf() for _ in range(KC)]
        b_y = [Buf() for _ in range(4)]
        b_gf = Buf()
        b_t = [Buf() for _ in range(NTMP)]
        b_w = [Buf() for _ in range(4)]
        b_bank = [Buf() for _ in range(8)]
        b_const = Buf()
        b_mk = Buf()
        b_inv = [Buf() for _ in range(4)]
        b_sm = [Buf() for _ in range(64)]
        b_eps = Buf()
        b_mxc = [[Buf() for _ in range(8)] for _ in range(2)]

        tr.op('dve', lambda e: e.memset(epsr, RMS_EPS), writes=[b_eps])
        tr.op('dve', lambda e: e.memset(epsl, LN_EPS), writes=[b_eps])
        cdeps = []
        cdeps.append(tr.dma('sp', lambda e: e.dma_start(out=pv, in_=pvec), 'cst', writes=[b_const]))
        cdeps.append(tr.dma('sp', lambda e: e.dma_start(out=cw.rearrange("p a b -> p (a b)"), in_=cwp), 'cst', writes=[b_const]))
        cdeps.append(tr.dma('sp', lambda e: e.dma_start(out=ones, in_=cst[:, 0:128]), 'cst', writes=[b_const]))
        cdeps.append(tr.dma('sp', lambda e: e.dma_start(out=idf, in_=cst[:, 128:256]), 'cst', writes=[b_const]))
        cdeps.append(tr.dma('sp', lambda e: e.dma_start(out=gf, in_=gfb), 'cst', writes=[b_gf]))
        cdeps = [cdeps[-1]]
        tr.op('act', lambda e: e.activation(out=ident, in_=idf, func=AF.Copy), reads=[b_const], writes=[b_const], extra=cdeps)

        xg = [view(65536, [D], F32), view(81920, [D], F32)]
        xs = [view(O_TMP + 16384, [D], BF16), view(98304, [D], BF16)]
        gn = view(O_TMP, [D], F32)
        hTs = [view(0, [KC, 512], BF16), view(32768, [KC, 512], BF16)]
        b_xg = [Buf(), Buf()]
        b_xs = [Buf(), Buf()]
        b_gn = Buf()
        b_hs = [[[Buf() for _ in range(8)] for _ in range(4)] for _ in range(2)]
        ss = sm[:, 0:33]
        sd = sm[:, 40:73]
        rs = sm[:, 80:113]
        b_ss = [Buf() for _ in range(NG)]
        b_sd = [Buf() for _ in range(NG)]
        b_rs = [Buf() for _ in range(NG)]
        tr.dma('sp', lambda e: e.dma_start(out=gn, in_=gnb), 'gn', writes=[b_gn])
        hst_deps = {}
        pend_hst = []

        def emit_hst(upto):
            while pend_hst and pend_hst[0][0] <= upto:
                _, stg, g0, ntok, rb = pend_hst.pop(0)
                hst_deps[stg] = tr.dma(
                    'sp', lambda e, stg=stg, g0=g0, ntok=ntok: e.dma_start(
                        out=hT_all[:, :, g0 * 128:g0 * 128 + ntok].rearrange("k p t -> p k t"),
                        in_=hTs[stg][:, :, 0:ntok]),
                    'hst%d' % stg, reads=rb)

        def stage_a(g):
            q = g % 2
            stg = (g // 4) % 2
            gl = g % 4
            tr.dma('sp', lambda e, g=g, q=q: e.dma_start(out=xg[q], in_=x_in[g * 128:(g + 1) * 128, :]),
                   'xg%d' % q, writes=[b_xg[q]])
            emit_hst(g - 2)
            tr.op('act', lambda e, g=g, q=q: e.activation(out=xs[q], in_=xg[q], func=AF.Square, accum_out=ss[:, g:g + 1]),
                  reads=[b_xg[q]], writes=[b_xs[q], b_ss[g]])
            tr.op('act', lambda e, g=g: e.activation(out=sd[:, g:g + 1], in_=ss[:, g:g + 1], func=AF.Sqrt,
                                                      scale=1.0 / D, bias=epsr),
                  reads=[b_ss[g], b_eps], writes=[b_sd[g]])
            tr.op('dve', lambda e, g=g: e.reciprocal(out=rs[:, g:g + 1], in_=sd[:, g:g + 1]),
                  reads=[b_sd[g]], writes=[b_rs[g]])
            tr.op('dve', lambda e, g=g, q=q: e.scalar_tensor_tensor(out=xs[q], in0=xg[q], scalar=rs[:, g:g + 1], in1=gn,
                                                                    op0=ALU.mult, op1=ALU.mult),
                  reads=[b_xg[q], b_rs[g], b_gn], writes=[b_xs[q]])

        def stage_b(g):
            q = g % 2
            stg = (g // 4) % 2
            gl = g % 4
            for kb in range(8):
                bank = kb % 4

                def tfn(e, q=q, kb=kb, bank=bank):
                    last = None
                    for qq in range(4):
                        last = e.transpose(out=PSb(bank)[:, qq * 128:(qq + 1) * 128],
                                           in_=xs[q][:, (kb * 4 + qq) * 128:(kb * 4 + qq + 1) * 128], identity=ident)
                    return last
                tr.op('pe', tfn, reads=[b_xs[q], b_const], writes=[b_bank[bank]])
                src = PSb(bank)[:, 0:512].rearrange("p (a b) -> p a b", b=128)
                dst = hTs[stg][:, kb * 4:kb * 4 + 4, gl * 128:(gl + 1) * 128]
                if kb % 2 == 0:
                    tr.op('act', lambda e, src=src, dst=dst: e.activation(out=dst, in_=src, func=AF.Copy),
                          reads=[b_bank[bank]], writes=[b_hs[stg][gl][kb]])
                else:
                    tr.op('dve', lambda e, src=src, dst=dst: e.tensor_copy(out=dst, in_=src),
                          reads=[b_bank[bank]], writes=[b_hs[stg][gl][kb]])
            if gl == 3 or g == NG - 1:
                ntok = (gl + 1) * 128
                g0 = g - gl
                rb = [b for glb in b_hs[stg][:gl + 1] for b in glb]
                pend_hst.append((g, stg, g0, ntok, rb))

        stage_a(0)
        for g in range(NG):
            if g + 1 < NG:
                stage_a(g + 1)
            stage_b(g)
        emit_hst(NG)
        pro_done = [hst_deps[0], hst_deps[1], ('act', tr.cnt['act']), ('dve', tr.cnt['dve']), ('pe', tr.cnt['pe'])]

        wstate = {'n': 0, 'ti': 0, 'm': 0}
        wb_pending = []
        wb_dep = {}
        first_bf = {}

        def flush_wb(upto):
            while wb_pending and wb_pending[0][0] <= upto:
                n, s, dst, nelem, ckey = wb_pending.pop(0)
                wb_dep[ckey] = tr.dma('sp', lambda e, s=s, dst=dst, nelem=nelem: e.dma_start(out=dst, in_=wslot[s][:, 0:nelem]),
                                      'wb%d' % s, reads=[b_w[s]])

        def wload(src_ap, nelem, ckey, f32_ap):
            ti = wstate['ti']
            n = wstate['n']
            s = n % 4
            wstate['n'] += 1
            flush_wb(n - 3)
            if ckey not in first_bf:
                if ckey[0] == 'in' and ckey[1] // 16 in (SEG_A, SEG_B):
                    first_bf[ckey] = 1
                else:
                    first_bf[ckey] = 2 + wstate['m'] % 2
                    wstate['m'] += 1
            if ti < first_bf[ckey]:
                tr.dma('pool', lambda e, s=s: e.dma_start(out=wslot[s][:, 0:nelem], in_=f32_ap), 'wc%d' % s,
                       writes=[b_w[s]])
                if ti == first_bf[ckey] - 1:
                    wb_pending.append((n, s, src_ap, nelem, ckey))
            else:
                tr.dma('sp', lambda e, s=s: e.dma_start(out=wslot[s][:, 0:nelem], in_=src_ap), 'w%d' % s,
                       writes=[b_w[s]], extra=[wb_dep[ckey]])
            return s

        pstate = {'n': 0}

        def next_banks():
            b = (0, 1) if pstate['n'] % 2 == 0 else (2, 3)
            pstate['n'] += 1
            return b

        def mm_pair(wtiles, KH, rhs_fn, rhs_bufs, banks, N, tile0, extra=()):
            nh = len(wtiles)
            dep = None
            for h, wtile in enumerate(wtiles):
                s = wload(*wtile)
                wt = wslot[s][:, 0:KH * 256].rearrange("p (k n) -> p k n", n=256)

                def fn(e, wt=wt, h=h):
                    last = None
                    for q in range(2):
                        for kc in range(KH):
                            last = e.matmul(PS[:, banks[q], 0:N], lhsT=wt[:, kc, q * 128:(q + 1) * 128],
                                            rhs=rhs_fn(h * KH + kc), start=(h == 0 and kc == 0),
                                            stop=(h == nh - 1 and kc == KH - 1))
                    return last
                dep = tr.op('pe', fn, reads=[b_w[s]] + list(rhs_bufs), writes=[b_bank[banks[0]], b_bank[banks[1]]],
                            extra=extra)
            return dep

        def inproj_pair(seg, pr, banks, halo, tile0, extra=(), Tj=T):
            grp = seg * 16 + pr
            wt = [(wq_in[grp, :, h, :], 4096, ('in', grp, h), w_in[grp, :, h, :]) for h in range(2)]
            if halo:
                return mm_pair(wt, 16, lambda kc: hT[:, kc, 0:Tj + 32], [b_hT], banks, Tj + 32, tile0, extra)
            return mm_pair(wt, 16, lambda kc: hT[:, kc, 16:16 + Tj], [b_hT], banks, Tj, tile0, extra)

        def proj_pair(wq, w_ap, name, pr, rhs_view, rhs_bufs, banks, tile0, Tj=T):
            wt = [(wq[pr, :, h, :], 4096, (name, pr, h), w_ap[pr, :, h, :]) for h in range(2)]
            return mm_pair(wt, 16, lambda kc: rhs_view[:, kc, 0:Tj], rhs_bufs, banks, Tj, tile0)

        prev_tile_done = []
        ring = {'n': 0}
        ystore_deps = []
        deferred = []

        def flush_stores():
            nonlocal ystore_deps
            if not deferred:
                return
            ystore_deps = []
            for (tg, nt, o0) in deferred:
                ystore_deps.append(tr.dma('sp', lambda e, tg=tg, nt=nt, o0=o0: e.dma_start(out=y[o0:o0 + nt, :], in_=ybuf[tg][0:nt, :]),
                                          'ys%d' % tg, reads=[b_y[tg]]))
            del deferred[:]

        for ti in range(ntiles):
            c, j = divmod(ti, NTC)
            base = c * CHR + j * T
            tile0 = (ti == 0)
            wstate['ti'] = ti
            Tj = 384 if j == NTC - 1 else T
            Wj = Tj + 32
            tx = list(pro_done) if tile0 else list(prev_tile_done)
            txg = list(pro_done) if tile0 else []

            def load_hT(dq, base, extra, Wj_):
                tr.dma(dq, lambda e, base=base, Wj_=Wj_, Tj=Tj: e.dma_start(out=hT[:, :, 0:Wj_], in_=hT_all[:, :, base:base + Wj_].rearrange("k p t -> p k t")),
                       'hT', writes=[b_hT], extra=extra)
                tr.dma(dq, lambda e, base=base, Wj_=Wj_, Tj=Tj: e.dma_start(out=mk[:, 0:Wj_], in_=maskb[:, base:base + Wj_]), 'mk', writes=[b_mk])
            if tile0:
                load_hT('sp', base, pro_done, Wj)

            m2, m4, m8, m16 = (tmpv[0][:, 0:Wj - 1], tmpv[1][:, 0:Wj - 3], tmpv[2][:, 0:Wj - 7], tmpv[3][:, 0:Wj - 15])
            tr.op('dve', lambda e, m2=m2, m4=m4, m8=m8, m16=m16, Tj=Tj, Wj=Wj: e.tensor_tensor(out=m2, in0=mk[:, 0:Wj - 1], in1=mk[:, 1:Wj], op=ALU.add),
                  reads=[b_mk], writes=[b_t[0]], extra=txg)
            tr.op('dve', lambda e, m2=m2, m4=m4, m8=m8, m16=m16, Tj=Tj, Wj=Wj: e.tensor_tensor(out=m4, in0=m2[:, 0:Wj - 3], in1=m2[:, 2:Wj - 1], op=ALU.add),
                  reads=[b_t[0]], writes=[b_t[1]], extra=txg)
            tr.op('dve', lambda e, m2=m2, m4=m4, m8=m8, m16=m16, Tj=Tj, Wj=Wj: e.tensor_tensor(out=m8, in0=m4[:, 0:Wj - 7], in1=m4[:, 4:Wj - 3], op=ALU.add),
                  reads=[b_t[1]], writes=[b_t[2]], extra=txg)
            tr.op('dve', lambda e, m2=m2, m4=m4, m8=m8, m16=m16, Tj=Tj, Wj=Wj: e.tensor_tensor(out=m16, in0=m8[:, 0:Wj - 15], in1=m8[:, 8:Wj - 7], op=ALU.add),
                  reads=[b_t[2]], writes=[b_t[3]], extra=txg)
            for gi, (srcm, off) in enumerate(((m2, 15), (m4, 14), (m8, 12), (m16, 8))):
                tr.op('dve', lambda e, gi=gi, srcm=srcm, off=off, Tj=Tj, Wj=Wj: e.tensor_scalar(
                    out=inv[gi][:, 0:Tj], in0=srcm[:, off:off + Tj], scalar1=1.0, scalar2=None, op0=ALU.max),
                    reads=[b_t[gi]], writes=[b_inv[gi]])
                tr.op('dve', lambda e, gi=gi, Tj=Tj, Wj=Wj: e.reciprocal(out=inv[gi][:, 0:Tj], in_=inv[gi][:, 0:Tj]), reads=[b_inv[gi]], writes=[b_inv[gi]])

            pending_stats = []

            def flush_stats(upto):
                while pending_stats and pending_stats[0] <= upto:
                    i = pending_stats.pop(0)
                    qq = i % 2
                    tr.op('pe', lambda e, i=i, Tj=Tj, Wj=Wj: e.matmul(PS[:, 6, 0:Tj], lhsT=ones, rhs=cbuf[:, i, 0:Tj], start=(i == 0), stop=(i == KC - 1)),
                          reads=[b_c[i], b_const], writes=[b_bank[6]])
                    tr.op('pe', lambda e, i=i, qq=qq, Tj=Tj, Wj=Wj: e.matmul(PS[:, 7, 0:Tj], lhsT=ones, rhs=tmpv[6 + qq][:, 0:Tj], start=(i == 0), stop=(i == KC - 1)),
                          reads=[b_t[6 + qq], b_const], writes=[b_bank[7]])

            def emit_sq(ip):
                for q in range(2):
                    i = 2 * ip + q
                    tr.op('act', lambda e, i=i, q=q, Tj=Tj, Wj=Wj: e.activation(out=tmpv[6 + q][:, 0:Tj], in_=cbuf[:, i, 0:Tj], func=AF.Square),
                          reads=[b_c[i]], writes=[b_t[6 + q]])
                    pending_stats.append(i)

            for ip in range(16):
                bb = next_banks()
                inproj_pair(SEG_B, ip, bb, True, tile0, extra=txg, Tj=Tj)
                for q in range(2):
                    tr.op('act', lambda e, q=q, bb=bb, Tj=Tj, Wj=Wj: e.activation(out=tmpv[q][:, 0:Wj], in_=PS[:, bb[q], 0:Wj], func=AF.Sigmoid),
                          reads=[b_bank[bb[q]]], writes=[b_t[q]], extra=txg)
                if ip > 0:
                    emit_sq(ip - 1)
                ab = next_banks()
                inproj_pair(SEG_A, ip, ab, True, tile0, extra=txg, Tj=Tj)
                if ip == 0:
                    flush_stores()
                    tx = tx + ystore_deps
                flush_stats(2 * ip - 1)
                vs = [2 + (2 * ip + q) % 4 for q in range(2)]
                for q in range(2):
                    tr.op('dve', lambda e, q=q, ab=ab, vs=vs, Tj=Tj, Wj=Wj: e.tensor_tensor(out=tmpv[vs[q]][:, 0:Wj], in0=PS[:, ab[q], 0:Wj], in1=tmpv[q][:, 0:Wj], op=ALU.mult),
                          reads=[b_bank[ab[q]], b_t[q]], writes=[b_t[vs[q]]], extra=txg)
                npool = 0 if ti < 1 else NPOOL
                kd = 31 - npool
                if npool:
                    for k in range(kd, 31):
                        for q in range(2):
                            i = 2 * ip + q
                            vin = tmpv[vs[q]][:, k + 1:k + 1 + Tj]
                            ppv = tmpv[12 + q][:, 0:Tj]
                            if k == kd:
                                tr.op('act', lambda e, ppv=ppv, vin=vin, i=i, k=k: e.activation(
                                    out=ppv, in_=vin, func=AF.Identity, scale=cw[:, i, k:k + 1]),
                                    reads=[b_t[vs[q]], b_const], writes=[b_t[12 + q]], extra=tx)
                            else:
                                r = 14 + ring['n'] % 3
                                ring['n'] += 1
                                ptv = tmpv[r][:, 0:Tj]
                                tr.op('act', lambda e, ptv=ptv, vin=vin, i=i, k=k: e.activation(
                                    out=ptv, in_=vin, func=AF.Identity, scale=cw[:, i, k:k + 1]),
                                    reads=[b_t[vs[q]], b_const], writes=[b_t[r]], extra=tx)
                                tr.op('pool', lambda e, ppv=ppv, ptv=ptv: e.tensor_tensor(out=ppv, in0=ppv, in1=ptv, op=ALU.add),
                                      reads=[b_t[12 + q], b_t[r]], writes=[b_t[12 + q]])
                for k in range(kd):
                    for q in range(2):
                        i = 2 * ip + q
                        acc = PS[:, 4 + q, 0:Tj]
                        vin = tmpv[vs[q]][:, k + 1:k + 1 + Tj]
                        if k == 0:
                            tr.op('dve', lambda e, acc=acc, vin=vin, i=i, Tj=Tj, Wj=Wj: e.tensor_scalar(
                                out=acc, in0=vin, scalar1=cw[:, i, 0:1], scalar2=pcol(PV_CB, i), op0=ALU.mult, op1=ALU.add),
                                reads=[b_t[vs[q]], b_const], writes=[b_bank[4 + q]], extra=txg)
                        elif k < 30:
                            tr.op('dve', lambda e, acc=acc, vin=vin, i=i, k=k, Tj=Tj, Wj=Wj: e.scalar_tensor_tensor(
                                out=acc, in0=vin, scalar=cw[:, i, k:k + 1], in1=acc, op0=ALU.mult, op1=ALU.add),
                                reads=[b_t[vs[q]], b_bank[4 + q]], writes=[b_bank[4 + q]])
                        else:
                            tr.op('dve', lambda e, acc=acc, vin=vin, i=i, k=k, Tj=Tj, Wj=Wj: e.scalar_tensor_tensor(
                                out=cbuf[:, i, 0:Tj], in0=vin, scalar=cw[:, i, k:k + 1], in1=acc, op0=ALU.mult, op1=ALU.add),
                                reads=[b_t[vs[q]], b_bank[4 + q]], writes=[b_c[i]], extra=tx)
                if npool:
                    for q in range(2):
                        i = 2 * ip + q
                        tr.op('dve', lambda e, q=q, i=i, Tj=Tj, Wj=Wj: e.tensor_tensor(out=cbuf[:, i, 0:Tj], in0=PS[:, 4 + q, 0:Tj], in1=tmpv[12 + q][:, 0:Tj], op=ALU.add),
                              reads=[b_bank[4 + q], b_t[12 + q]], writes=[b_c[i]], extra=tx)

            def emit_z(ip):
                zb = next_banks()
                inproj_pair(SEG_ZC, ip, zb, False, tile0, Tj=Tj)
                for q in range(2):
                    szs = 8 + 2 * (ip % 2) + q
                    tr.op('act', lambda e, q=q, zb=zb, szs=szs, Tj=Tj, Wj=Wj: e.activation(out=tmpv[szs][:, 0:Tj], in_=PS[:, zb[q], 0:Tj], func=AF.Silu),
                          reads=[b_bank[zb[q]]], writes=[b_t[szs]], extra=tx)
            emit_z(0)
            emit_z(1)
            emit_sq(15)
            flush_stats(KC)

            tA, tB = tmpv[0][:, 0:Tj], tmpv[1][:, 0:Tj]
            tr.op('dve', lambda e, tA=tA, tB=tB, Tj=Tj, Wj=Wj: e.tensor_scalar(out=tA, in0=PS[:, 6, 0:Tj], scalar1=1.0 / D, scalar2=None, op0=ALU.mult),
                  reads=[b_bank[6]], writes=[b_t[0]])
            tr.op('dve', lambda e, tA=tA, tB=tB, Tj=Tj, Wj=Wj: e.tensor_tensor(out=tB, in0=tA, in1=tA, op=ALU.mult), reads=[b_t[0]], writes=[b_t[1]])
            tr.op('dve', lambda e, tA=tA, tB=tB, Tj=Tj, Wj=Wj: e.scalar_tensor_tensor(out=tB, in0=PS[:, 7, 0:Tj], scalar=1.0 / D, in1=tB, op0=ALU.mult, op1=ALU.subtract),
                  reads=[b_bank[7], b_t[1]], writes=[b_t[1]])
            tr.op('act', lambda e, tA=tA, tB=tB, Tj=Tj, Wj=Wj: e.activation(out=tB, in_=tB, func=AF.Sqrt, bias=epsl, scale=1.0), reads=[b_t[1], b_eps], writes=[b_t[1]])
            tr.op('dve', lambda e, tA=tA, tB=tB, Tj=Tj, Wj=Wj: e.reciprocal(out=PS[:, 6, 0:Tj], in_=tB), reads=[b_t[1]], writes=[b_bank[6]])
            tr.op('dve', lambda e, tA=tA, tB=tB, Tj=Tj, Wj=Wj: e.scalar_tensor_tensor(out=PS[:, 7, 0:Tj], in0=tA, scalar=-1.0, in1=PS[:, 6, 0:Tj], op0=ALU.mult, op1=ALU.mult),
                  reads=[b_t[0], b_bank[6]], writes=[b_bank[7]])

            for ip in range(16):
                if ip >= 2:
                    emit_z(ip)
                for q in range(2):
                    i = 2 * ip + q
                    tr.op('dve', lambda e, q=q, i=i, Tj=Tj, Wj=Wj: e.tensor_tensor(out=tmpv[4 + q][:, 0:Tj], in0=cbuf[:, i, 0:Tj], in1=PS[:, 6, 0:Tj], op=ALU.mult),
                          reads=[b_c[i], b_bank[6]], writes=[b_t[4 + q]])
                for q in range(2):
                    tr.op('dve', lambda e, q=q, Tj=Tj, Wj=Wj: e.tensor_tensor(out=tmpv[4 + q][:, 0:Tj], in0=tmpv[4 + q][:, 0:Tj], in1=PS[:, 7, 0:Tj], op=ALU.add),
                          reads=[b_t[4 + q], b_bank[7]], writes=[b_t[4 + q]])
                for q in range(2):
                    i = 2 * ip + q
                    tr.op('act', lambda e, q=q, i=i, Tj=Tj, Wj=Wj: e.activation(out=tmpv[6 + q][:, 0:Tj], in_=tmpv[4 + q][:, 0:Tj], func=AF.Silu,
                                                                  scale=pcol(PV_LNG, i), bias=pcol(PV_LNB, i)),
                          reads=[b_t[4 + q], b_const], writes=[b_t[6 + q]])
                for q in range(2):
                    i = 2 * ip + q
                    szs = 8 + 2 * (ip % 2) + q
                    tr.op('dve', lambda e, q=q, i=i, szs=szs, Tj=Tj, Wj=Wj: e.tensor_tensor(out=yc_in[:, i, 0:Tj], in0=tmpv[6 + q][:, 0:Tj], in1=tmpv[szs][:, 0:Tj], op=ALU.mult),
                          reads=[b_t[6 + q], b_t[szs]], writes=[b_yc[i]], extra=tx)
            c_done = [('dve', tr.cnt['dve']), ('pe', tr.cnt['pe']), ('act', tr.cnt['act'])]

            mx = [view(O_TMP + 9 * 1792, [8, T], BF16), view(O_TMP + 13 * 1792, [8, T], BF16)]
            for g in range(4):
                kwin = (2, 4, 8, 16)[g]
                mxi = g % 2
                for up in range(4):
                    ub = next_banks()
                    inproj_pair(SEG_U, g * 4 + up, ub, True, tile0, Tj=Tj)
                    for q in range(2):
                        tr.op('act', lambda e, q=q, ub=ub, Tj=Tj, Wj=Wj: e.activation(out=tmpv[q][:, 0:Wj], in_=PS[:, ub[q], 0:Wj], func=AF.Copy),
                              reads=[b_bank[ub[q]]], writes=[b_t[q]])
                    for q in range(2):
                        tr.op('dve', lambda e, q=q, ub=ub, Tj=Tj, Wj=Wj: e.tensor_tensor(out=tmpv[2 + q][:, 0:Wj - 1], in0=PS[:, ub[q], 0:Wj - 1], in1=tmpv[q][:, 1:Wj], op=ALU.add),
                              reads=[b_bank[ub[q]], b_t[q]], writes=[b_t[2 + q]])
                    cur = [2, 3]
                    off = 15
                    ln = Wj - 1
                    for step, sh in ((4, 2), (8, 4), (16, 8)):
                        if kwin < step:
                            break
                        nxt = [4, 5] if cur[0] == 2 else [2, 3]
                        nl = ln - sh
                        for q in range(2):
                            tr.op('dve', lambda e, q=q, cur=cur, nxt=nxt, nl=nl, sh=sh, Tj=Tj, Wj=Wj: e.tensor_tensor(
                                out=tmpv[nxt[q]][:, 0:nl], in0=tmpv[cur[q]][:, 0:nl], in1=tmpv[cur[q]][:, sh:sh + nl], op=ALU.add),
                                reads=[b_t[cur[q]]], writes=[b_t[nxt[q]]])
                        cur = nxt
                        ln = nl
                        off -= sh // 2
                    for q in range(2):
                        tr.op('dve', lambda e, q=q, cur=cur, off=off, g=g, Tj=Tj, Wj=Wj: e.tensor_tensor(
                            out=tmpv[6 + q][:, 0:Tj], in0=tmpv[cur[q]][:, off:off + Tj], in1=inv[g][:, 0:Tj], op=ALU.mult),
                            reads=[b_t[cur[q]], b_inv[g]], writes=[b_t[6 + q]])
                    for q in range(2):
                        tr.op('dve', lambda e, q=q, up=up, mxi=mxi, Tj=Tj, Wj=Wj: e.tensor_tensor(
                            out=mx[mxi][:, 2 * up + q, 0:Tj], in0=tmpv[6 + q][:, 0:Tj], in1=tmpv[q][:, 16:16 + Tj], op=ALU.subtract),
                            reads=[b_t[6 + q], b_t[q]], writes=[b_mxc[mxi][2 * up + q]], extra=tx)
                for opi in range(4):
                    zb = next_banks()
                    inproj_pair(SEG_ZP, g * 4 + opi, zb, False, tile0, Tj=Tj)
                    for q in range(2):
                        tr.op('act', lambda e, q=q, zb=zb, Tj=Tj, Wj=Wj: e.activation(out=tmpv[7 - q][:, 0:Tj], in_=PS[:, zb[q], 0:Tj], func=AF.Silu),
                              reads=[b_bank[zb[q]]], writes=[b_t[7 - q]])
                    wb = next_banks()
                    mm_pair([(wq_pool[g, opi], 2048, ('pool', g, opi), w_pool[g, opi])], 8,
                            lambda kc, mxi=mxi, Tj=Tj: mx[mxi][:, kc, 0:Tj], b_mxc[mxi], wb, Tj, tile0)
                    for q in range(2):
                        i = g * 8 + 2 * opi + q
                        tr.op('dve', lambda e, q=q, i=i, wb=wb, Tj=Tj, Wj=Wj: e.scalar_tensor_tensor(
                            out=yp_in[:, i, 0:Tj], in0=PS[:, wb[q], 0:Tj], scalar=pcol(PV_LS, i), in1=tmpv[7 - q][:, 0:Tj], op0=ALU.mult, op1=ALU.mult),
                            reads=[b_bank[wb[q]], b_t[7 - q], b_const], writes=[b_yp[i]], extra=c_done + tx)

            for dp in range(16):
                gb = next_banks()
                inproj_pair(SEG_GP, dp, gb, False, tile0, Tj=Tj)
                for q in range(2):
                    tr.op('act', lambda e, q=q, gb=gb, Tj=Tj, Wj=Wj: e.activation(out=tmpv[q][:, 0:Tj], in_=PS[:, gb[q], 0:Tj], func=AF.Sigmoid),
                          reads=[b_bank[gb[q]]], writes=[b_t[q]])
                gcb = next_banks()
                inproj_pair(SEG_GC, dp, gcb, False, tile0, Tj=Tj)
                for q in range(2):
                    tr.op('act', lambda e, q=q, gcb=gcb, Tj=Tj, Wj=Wj: e.activation(out=tmpv[2 + q][:, 0:Tj], in_=PS[:, gcb[q], 0:Tj], func=AF.Sigmoid),
                          reads=[b_bank[gcb[q]]], writes=[b_t[2 + q]])
                pb = next_banks()
                proj_pair(wq_pp, w_pp, 'pp', dp, yp_in, b_yp, pb, tile0, Tj=Tj)
                for q in range(2):
                    tr.op('dve', lambda e, q=q, pb=pb, Tj=Tj, Wj=Wj: e.tensor_tensor(out=tmpv[4 + q][:, 0:Tj], in0=PS[:, pb[q], 0:Tj], in1=tmpv[q][:, 0:Tj], op=ALU.mult),
                          reads=[b_bank[pb[q]], b_t[q]], writes=[b_t[4 + q]])
                cb_ = next_banks()
                proj_pair(wq_pc, w_pc, 'pc', dp, yc_in, b_yc, cb_, tile0, Tj=Tj)
                for q in range(2):
                    i = 2 * dp + q
                    tr.op('dve', lambda e, q=q, i=i, cb_=cb_, Tj=Tj, Wj=Wj: e.scalar_tensor_tensor(
                        out=tmpv[6 + q][:, 0:Tj], in0=PS[:, cb_[q], 0:Tj], scalar=pcol(PV_BPC, i), in1=tmpv[2 + q][:, 0:Tj], op0=ALU.add, op1=ALU.mult),
                        reads=[b_bank[cb_[q]], b_t[2 + q], b_const], writes=[b_t[6 + q]])
                for q in range(2):
                    i = 2 * dp + q
                    tr.op('dve', lambda e, q=q, i=i, Tj=Tj, Wj=Wj: e.tensor_tensor(out=merged[:, i, 0:Tj], in0=tmpv[4 + q][:, 0:Tj], in1=tmpv[6 + q][:, 0:Tj], op=ALU.add),
                          reads=[b_t[4 + q], b_t[6 + q]], writes=[b_mg[i]], extra=c_done)
            p4_done = [('pe', tr.cnt['pe']), ('dve', tr.cnt['dve']), ('act', tr.cnt['act'])]

            if ti + 1 < ntiles:
                c2, j2 = divmod(ti + 1, NTC)
                load_hT('sp', c2 * CHR + j2 * T, [], (384 if j2 == NTC - 1 else T) + 32)

            ntoks = [128, 128, 128, 32]
            ngrp = 3 if j == NTC - 1 else 4
            ssq = [sm[:, 128 + tg * 8:128 + tg * 8 + 8] for tg in range(4)]
            for dblk in range(8):
                bset = (dblk % 2) * 4
                for kq in range(4):
                    s = wload(wq_out[dblk, :, kq, :], 4096, ('out', dblk, kq), w_out[dblk, :, kq, :])
                    wt = wslot[s].rearrange("p (k n) -> p k n", n=512)

                    def fn(e, wt=wt, kq=kq, bset=bset, ngrp=ngrp):
                        last = None
                        for tg in range(ngrp):
                            nt = ntoks[tg]
                            for kc in range(8):
                                last = e.matmul(PS[0:nt, bset + tg, :], lhsT=merged[:, kq * 8 + kc, tg * 128:tg * 128 + nt],
                                                rhs=wt[:, kc, :], start=(kq == 0 and kc == 0), stop=(kq == 3 and kc == 7))
                        return last
                    tr.op('pe', fn, reads=[b_w[s]] + b_mg, writes=[b_bank[bset + tg] for tg in range(ngrp)])
                if dblk == 0:
                  for tg in range(ngrp):
                    r0 = base + 16 + tg * 128
                    nt = ntoks[tg]
                    tr.dma('sp', lambda e, tg=tg, r0=r0, nt=nt, Tj=Tj, Wj=Wj: e.dma_start(out=ybuf[tg][0:nt, :], in_=x_in[r0:r0 + nt, :]),
                           'yl%d' % tg, writes=[b_y[tg]], extra=p4_done + ystore_deps)
                for tg in range(ngrp):
                    nt = ntoks[tg]
                    ysl = ybuf[tg][0:nt, dblk * 512:(dblk + 1) * 512]
                    tr.op('dve', lambda e, tg=tg, nt=nt, ysl=ysl, bset=bset, Tj=Tj, Wj=Wj: e.tensor_tensor(out=ysl, in0=PS[0:nt, bset + tg, :], in1=ysl, op=ALU.add),
                          reads=[b_bank[bset + tg], b_y[tg]], writes=[b_y[tg]])
                    tr.op('act', lambda e, tg=tg, nt=nt, ysl=ysl, dblk=dblk, Tj=Tj, Wj=Wj: e.activation(
                        out=junkv[tg % 2][0:nt, :], in_=ysl, func=AF.Square, accum_out=ssq[tg][0:nt, dblk:dblk + 1]),
                        reads=[b_y[tg]], writes=[b_t[4 + 2 * (tg % 2)], b_t[5 + 2 * (tg % 2)], b_sm[tg * 8 + dblk]], extra=p4_done)
            for tg in range(ngrp):
                nt = ntoks[tg]
                tot = sm[0:nt, 192 + tg:193 + tg]
                tr.op('dve', lambda e, tg=tg, nt=nt, tot=tot, Tj=Tj, Wj=Wj: e.tensor_reduce(out=tot, in_=ssq[tg][0:nt, :], axis=mybir.AxisListType.X, op=ALU.add),
                      reads=[b_sm[tg * 8 + d_] for d_ in range(8)], writes=[b_sm[40 + tg]])
                tr.op('act', lambda e, nt=nt, tot=tot, Tj=Tj, Wj=Wj: e.activation(out=tot, in_=tot, func=AF.Sqrt, scale=1.0 / D, bias=epsr[0:nt, :]),
                      reads=[b_sm[40 + tg], b_eps], writes=[b_sm[40 + tg]])
                tr.op('dve', lambda e, tot=tot, Tj=Tj, Wj=Wj: e.reciprocal(out=tot, in_=tot), reads=[b_sm[40 + tg]], writes=[b_sm[40 + tg]])
                tr.op('dve', lambda e, tg=tg, nt=nt, tot=tot, Tj=Tj, Wj=Wj: e.scalar_tensor_tensor(
                    out=ybuf[tg][0:nt, :], in0=ybuf[tg][0:nt, :], scalar=tot, in1=gf[0:nt, :], op0=ALU.mult, op1=ALU.mult),
                    reads=[b_y[tg], b_sm[40 + tg], b_gf], writes=[b_y[tg]])
                o0 = c * 2048 + j * T + tg * 128
                deferred.append((tg, nt, o0))
            prev_tile_done = [('pe', tr.cnt['pe']), ('dve', tr.cnt['dve']), ('act', tr.cnt['act'])]
            flush_wb(1 << 60)

        flush_stores()
        tr.wait('sp', ystore_deps + [(k, v) for k, v in tr.dcnt.items() if k.startswith('ys') or k.startswith('wb')])

        for k in list(tr.dcnt.keys()):
            sem(k)
        with nc.Block() as block:
            def run(eng_obj, key):
                for o in tr.ops[key]:
                    if o[0] == 'w':
                        eng_obj.wait_ge(sem(o[1]), o[2])
                    elif o[0] == 'o':
                        o[1](eng_obj).then_inc(sem(key), 1)
                    else:
                        o[1](eng_obj).then_inc(sem(o[2]), 16)

            @block.sync
            def _(e):
                run(e, 'sp')

            @block.gpsimd
            def _(e):
                run(e, 'pool')

            @block.tensor
            def _(e):
                run(e, 'pe')

            @block.scalar
            def _(e):
                run(e, 'act')

            @block.vector
            def _(e):
                run(e, 'dve')
    return nc


def _prep_core(i, xp, xsm):
    xin = np.zeros((ROWS, D), np.float32)
    mask = np.zeros((ROWS,), np.float32)
    for c in range(NCH):
        if c == 0:
            seq, s0 = xp[i], 0
        else:
            seq, s0 = xsm[i // 2], (i % 2) * 2048
        L = seq.shape[0]
        lo = s0 - 16
        hi = s0 + CHR - 16
        a, b = max(lo, 0), min(hi, L)
        xin[c * CHR + (a - lo): c * CHR + (b - lo)] = seq[a:b]
        mask[c * CHR + (a - lo): c * CHR + (b - lo)] = 1.0
    return xin, np.ascontiguousarray(np.broadcast_to(mask[None, :], (128, ROWS)))


_NC_CACHE = {}


def _prep_shared(g_norm, w_in, w_pool, ls_pool, w_proj_pool, conv_w, conv_b, ln_g, ln_b, w_proj_conv, b_proj_conv,
                 w_out, g_final):
    f = lambda a: np.ascontiguousarray(np.asarray(a, dtype=np.float32))
    pvec = np.zeros((128, 160), np.float32)
    for jv, vec in enumerate((ls_pool, conv_b, ln_g, ln_b, b_proj_conv)):
        pvec[:, jv * 32:(jv + 1) * 32] = f(vec).reshape(32, 128).T
    cwp = np.ascontiguousarray(f(conv_w).reshape(31, 32, 128).transpose(2, 1, 0).reshape(128, 992))
    cst = np.concatenate([np.ones((128, 128), np.float32), np.eye(128, dtype=np.float32)], axis=1)
    blk = lambda w, ng: np.ascontiguousarray(w.reshape(2, 16, 128, ng, 256).transpose(3, 2, 0, 1, 4)).reshape(ng, 128, 2, 4096)
    return {
        "w_in": blk(f(w_in).reshape(D, NIN), 112),
        "w_pool": np.ascontiguousarray(f(w_pool).reshape(4, 8, 128, 4, 256).transpose(0, 3, 2, 1, 4)).reshape(4, 4, 128, 2048),
        "w_pp": blk(f(w_proj_pool).reshape(D, D), 16), "w_pc": blk(f(w_proj_conv).reshape(D, D), 16),
        "w_out": np.ascontiguousarray(f(w_out).reshape(4, 8, 128, 8, 512).transpose(3, 2, 0, 1, 4)).reshape(8, 128, 4, 4096),
        "gnb": np.ascontiguousarray(np.broadcast_to(f(g_norm).reshape(1, D), (128, D))),
        "gfb": np.ascontiguousarray(np.broadcast_to(f(g_final).reshape(1, D), (128, D))),
        "pvec": pvec, "cwp": cwp, "cst": cst,
    }


def kernel(x_prompt, x_sample, g_norm, w_in, w_pool, ls_pool, w_proj_pool, conv_w, conv_b, ln_g, ln_b,
           w_proj_conv, b_proj_conv, w_out, g_final):
    f = lambda a: np.ascontiguousarray(np.asarray(a, dtype=np.float32))
    xp, xsm = f(x_prompt), f(x_sample)
    shared = _prep_shared(g_norm, w_in, w_pool, ls_pool, w_proj_pool, conv_w, conv_b, ln_g, ln_b, w_proj_conv,
                          b_proj_conv, w_out, g_final)
    in_maps = []
    for i in range(8):
        xin, mb = _prep_core(i, xp, xsm)
        m = dict(shared)
        m["x_in"] = xin
        m["maskb"] = mb
        in_maps.append(m)
    if "nc" not in _NC_CACHE:
        _NC_CACHE["nc"] = build_nc()
    res = run_bass_kernel_spmd(_NC_CACHE["nc"], in_maps, core_ids=list(range(8)))
    y_prompt = np.empty((8, 2048, D), np.float32)
    y_sample = np.empty((4, 4096, D), np.float32)
    for i in range(8):
        yo = np.asarray(res.results[i]["y"], dtype=np.float32).reshape(NCH * 2048, D)
        y_prompt[i] = yo[0:2048]
        s0 = (i % 2) * 2048
        y_sample[i // 2, s0:s0 + 2048] = yo[2048:4096]
    return (y_prompt, y_sample)
```

```python
import contextlib
import numpy as np
import concourse.bass as bass
import concourse.mybir as mybir
from concourse.bass_utils import run_bass_kernel_spmd
from concourse.alu_op_type import AluOpType as ALU

F32 = mybir.dt.float32
BF16 = mybir.dt.bfloat16
F32R = mybir.dt.float32r
AF = mybir.ActivationFunctionType

D = 4096
KC = 32
T = 416
WIN = 448
NTC = 5
CHR = 2112
NCH = 2
ROWS = NCH * CHR
NG = ROWS // 128
NIN = 28672
SEG_U, SEG_ZP, SEG_A, SEG_B, SEG_ZC, SEG_GP, SEG_GC = range(7)
RMS_EPS = 1e-6
NCONV = 168
NPOOL = 12
LN_EPS = 1e-5
ENG = ('pe', 'act', 'dve', 'sp', 'pool')

O_RH = 0
O_BIG = O_RH + 28672
O_RA = O_BIG
O_RB = O_BIG + 53248
O_GF = O_BIG + 79872
O_TMP = O_GF + 16384
NTMP = 17
O_WR = O_TMP + NTMP * 1792 + 256
O_ONES = O_WR + 32768
O_IDENT = O_ONES + 512
O_IDF = O_IDENT + 256
O_PV = O_IDF + 512
O_CW = O_PV + 640
O_MK = O_CW + 3968
O_INV = O_MK + 1792
O_SM = O_INV + 4 * 1664
O_END = O_SM + 1024


class Buf:
    __slots__ = ('w', 'r')

    def __init__(self):
        self.w = None
        self.r = {}


class Tr:
    def __init__(self):
        self.ops = {e: [] for e in ENG}
        self.cnt = {e: 0 for e in ENG}
        self.waited = {e: {} for e in ENG}
        self.dcnt = {}

    def wait(self, eng, deps):
        best = {}
        for d in deps:
            if d is None:
                continue
            k, v = d
            if v > best.get(k, 0):
                best[k] = v
        for k, v in best.items():
            if k == eng and eng == 'pe':
                continue
            if self.waited[eng].get(k, 0) >= v:
                continue
            self.waited[eng][k] = v
            self.ops[eng].append(('w', k, v))

    def _deps(self, reads, writes, extra):
        deps = list(extra)
        for b in reads:
            deps.append(b.w)
        for b in writes:
            deps.append(b.w)
            deps.extend(b.r.items())
        return deps

    def _upd(self, me, reads, writes):
        for b in reads:
            if b.r.get(me[0], 0) < me[1]:
                b.r[me[0]] = me[1]
        for b in writes:
            b.w = me
            b.r = {}

    def op(self, eng, fn, reads=(), writes=(), extra=()):
        self.wait(eng, self._deps(reads, writes, extra))
        self.cnt[eng] += 1
        me = (eng, self.cnt[eng])
        self.ops[eng].append(('o', fn))
        self._upd(me, reads, writes)
        return me

    def dma(self, eng, fn, semkey, reads=(), writes=(), extra=()):
        self.wait(eng, self._deps(reads, writes, extra))
        self.dcnt[semkey] = self.dcnt.get(semkey, 0) + 16
        me = (semkey, self.dcnt[semkey])
        self.ops[eng].append(('d', fn, semkey))
        self._upd(me, reads, writes)
        return me


def build_nc(ntiles=NCH * NTC):
    nc = bass.Bass("TRN2", target_bir_lowering=False)
    dt = nc.dram_tensor
    x_in = dt("x_in", [ROWS, D], F32, kind="ExternalInput").ap()
    maskb = dt("maskb", [128, ROWS], F32, kind="ExternalInput").ap()
    w_in = dt("w_in", [112, 128, 2, 4096], F32, kind="ExternalInput").ap()
    w_pool = dt("w_pool", [4, 4, 128, 2048], F32, kind="ExternalInput").ap()
    w_pp = dt("w_pp", [16, 128, 2, 4096], F32, kind="ExternalInput").ap()
    w_pc = dt("w_pc", [16, 128, 2, 4096], F32, kind="ExternalInput").ap()
    w_out = dt("w_out", [8, 128, 4, 4096], F32, kind="ExternalInput").ap()
    gnb = dt("gnb", [128, D], F32, kind="ExternalInput").ap()
    gfb = dt("gfb", [128, D], F32, kind="ExternalInput").ap()
    pvec = dt("pvec", [128, 160], F32, kind="ExternalInput").ap()
    cwp = dt("cwp", [128, 992], F32, kind="ExternalInput").ap()
    cst = dt("cst", [128, 256], F32, kind="ExternalInput").ap()
    y = dt("y", [NCH * 2048, D], F32, kind="ExternalOutput").ap()
    hT_all = dt("hT_all", [KC, 128, ROWS], BF16, kind="Internal").ap()
    wq_in = dt("wq_in", [112, 128, 2, 4096], BF16, kind="Internal").ap()
    wq_pool = dt("wq_pool", [4, 4, 128, 2048], BF16, kind="Internal").ap()
    wq_pp = dt("wq_pp", [16, 128, 2, 4096], BF16, kind="Internal").ap()
    wq_pc = dt("wq_pc", [16, 128, 2, 4096], BF16, kind="Internal").ap()
    wq_out = dt("wq_out", [8, 128, 4, 4096], BF16, kind="Internal").ap()

    tr = Tr()

    with contextlib.ExitStack() as es:
        S = es.enter_context(nc.sbuf_tensor("S", [128, O_END // 4], F32))
        PS = es.enter_context(nc.psum_tensor("PS", [128, 8, 512], F32))
        sems = {}

        def sem(key):
            if key not in sems:
                sems[key] = es.enter_context(nc.semaphore("s_" + key))
            return sems[key]

        for e in ('pe', 'act', 'dve', 'pool'):
            sem(e)

        def view(off, shape, dtype):
            n = int(np.prod(shape))
            esz = 4 if dtype == F32 else 2
            ap = S[:, off // 4: off // 4 + (n * esz + 3) // 4]
            if dtype != F32:
                ap = ap.bitcast(dtype)
            if len(shape) == 2:
                ap = ap.rearrange("p (a b) -> p a b", b=shape[1])
            return ap

        def PSb(bank):
            return PS[:, bank, :].bitcast(BF16)

        hT = view(O_RH, [KC, WIN], BF16)
        cbuf = view(O_RA, [KC, T], F32)
        merged = view(O_RA, [KC, T], BF16)
        yp_in = view(O_RA + 26624, [KC, T], BF16)
        yc_in = view(O_RB, [KC, T], BF16)
        ybuf = [view(O_RA + 26624 + i * 16384, [D], F32) for i in range(3)] + [view(O_TMP + 8 * 1792, [D], F32)]
        gf = view(O_GF, [D], F32)
        tmpv = [view(O_TMP + i * 1792, [WIN], F32) for i in range(NTMP)]
        wslot = [view(O_WR + i * 8192, [4096], BF16) for i in range(4)]
        ones = view(O_ONES, [128], F32)
        ident = view(O_IDENT, [128], BF16)
        idf = view(O_IDF, [128], F32)
        pv = view(O_PV, [160], F32)
        cw = view(O_CW, [KC, 31], F32)
        mk = view(O_MK, [WIN], F32)
        inv = [view(O_INV + i * 1664, [T], F32) for i in range(4)]
        sm = view(O_SM, [256], F32)
        epsr = sm[:, 200:201]
        epsl = sm[:, 201:202]
        junkv = [view(O_TMP + 4 * 1792, [512], F32), view(O_TMP + 6 * 1792, [512], F32)]

        def pcol(j, i):
            return pv[:, j * 32 + i: j * 32 + i + 1]
        PV_LS, PV_CB, PV_LNG, PV_LNB, PV_BPC = range(5)

        b_hT = Buf()
        b_c = [Buf() for _ in range(KC)]
        b_yc = [Buf() for _ in range(KC)]
        b_yp = [Buf() for _ in range(KC)]
        b_mg = [Buf() for _ in range(KC)]
        b_y = [Buf() for _ in range(4)]
        b_gf = Buf()
        b_t = [Buf() for _ in range(NTMP)]
        b_w = [Buf() for _ in range(4)]
        b_bank = [Buf() for _ in range(8)]
        b_const = Buf()
        b_mk = Buf()
        b_inv = [Buf() for _ in range(4)]
        b_sm = [Buf() for _ in range(64)]
        b_eps = Buf()
        b_mxc = [[Buf() for _ in range(8)] for _ in range(2)]

        tr.op('dve', lambda e: e.memset(epsr, RMS_EPS), writes=[b_eps])
        tr.op('dve', lambda e: e.memset(epsl, LN_EPS), writes=[b_eps])
        cdeps = []
        cdeps.append(tr.dma('sp', lambda e: e.dma_start(out=pv, in_=pvec), 'cst', writes=[b_const]))
        cdeps.append(tr.dma('sp', lambda e: e.dma_start(out=cw.rearrange("p a b -> p (a b)"), in_=cwp), 'cst', writes=[b_const]))
        cdeps.append(tr.dma('sp', lambda e: e.dma_start(out=ones, in_=cst[:, 0:128]), 'cst', writes=[b_const]))
        cdeps.append(tr.dma('sp', lambda e: e.dma_start(out=idf, in_=cst[:, 128:256]), 'cst', writes=[b_const]))
        cdeps.append(tr.dma('sp', lambda e: e.dma_start(out=gf, in_=gfb), 'cst', writes=[b_gf]))
        cdeps = [cdeps[-1]]
        tr.op('act', lambda e: e.activation(out=ident, in_=idf, func=AF.Copy), reads=[b_const], writes=[b_const], extra=cdeps)

        xg = [view(65536, [D], F32), view(81920, [D], F32)]
        xs = [view(O_TMP + 16384, [D], BF16), view(98304, [D], BF16)]
        gn = view(O_TMP, [D], F32)
        hTs = [view(0, [KC, 512], BF16), view(32768, [KC, 512], BF16)]
        b_xg = [Buf(), Buf()]
        b_xs = [Buf(), Buf()]
        b_gn = Buf()
        b_hs = [[[Buf() for _ in range(8)] for _ in range(4)] for _ in range(2)]
        ss = sm[:, 0:33]
        sd = sm[:, 40:73]
        rs = sm[:, 80:113]
        b_ss = [Buf() for _ in range(NG)]
        b_sd = [Buf() for _ in range(NG)]
        b_rs = [Buf() for _ in range(NG)]
        tr.dma('sp', lambda e: e.dma_start(out=gn, in_=gnb), 'gn', writes=[b_gn])
        hst_deps = {}
        pend_hst = []

        def emit_hst(upto):
            while pend_hst and pend_hst[0][0] <= upto:
                _, stg, g0, ntok, rb = pend_hst.pop(0)
                hst_deps[stg] = tr.dma(
                    'sp', lambda e, stg=stg, g0=g0, ntok=ntok: e.dma_start(
                        out=hT_all[:, :, g0 * 128:g0 * 128 + ntok].rearrange("k p t -> p k t"),
                        in_=hTs[stg][:, :, 0:ntok]),
                    'hst%d' % stg, reads=rb)

        def stage_a(g):
            q = g % 2
            stg = (g // 4) % 2
            gl = g % 4
            tr.dma('sp', lambda e, g=g, q=q: e.dma_start(out=xg[q], in_=x_in[g * 128:(g + 1) * 128, :]),
                   'xg%d' % q, writes=[b_xg[q]])
            emit_hst(g - 2)
            tr.op('act', lambda e, g=g, q=q: e.activation(out=xs[q], in_=xg[q], func=AF.Square, accum_out=ss[:, g:g + 1]),
                  reads=[b_xg[q]], writes=[b_xs[q], b_ss[g]])
            tr.op('act', lambda e, g=g: e.activation(out=sd[:, g:g + 1], in_=ss[:, g:g + 1], func=AF.Sqrt,
                                                      scale=1.0 / D, bias=epsr),
                  reads=[b_ss[g], b_eps], writes=[b_sd[g]])
            tr.op('dve', lambda e, g=g: e.reciprocal(out=rs[:, g:g + 1], in_=sd[:, g:g + 1]),
                  reads=[b_sd[g]], writes=[b_rs[g]])
            tr.op('dve', lambda e, g=g, q=q: e.scalar_tensor_tensor(out=xs[q], in0=xg[q], scalar=rs[:, g:g + 1], in1=gn,
                                                                    op0=ALU.mult, op1=ALU.mult),
                  reads=[b_xg[q], b_rs[g], b_gn], writes=[b_xs[q]])

        def stage_b(g):
            q = g % 2
            stg = (g // 4) % 2
            gl = g % 4
            for kb in range(8):
                bank = kb

                def tfn(e, q=q, kb=kb, bank=bank):
                    last = None
                    for qq in range(4):
                        last = e.transpose(out=PSb(bank)[:, qq * 128:(qq + 1) * 128],
                                           in_=xs[q][:, (kb * 4 + qq) * 128:(kb * 4 + qq + 1) * 128], identity=ident)
                    return last
                tr.op('pe', tfn, reads=[b_xs[q], b_const], writes=[b_bank[bank]])
                src = PSb(bank)[:, 0:512].rearrange("p (a b) -> p a b", b=128)
                dst = hTs[stg][:, kb * 4:kb * 4 + 4, gl * 128:(gl + 1) * 128]
                if kb % 2 == 0:
                    tr.op('act', lambda e, src=src, dst=dst: e.activation(out=dst, in_=src, func=AF.Copy),
                          reads=[b_bank[bank]], writes=[b_hs[stg][gl][kb]])
                else:
                    tr.op('dve', lambda e, src=src, dst=dst: e.tensor_copy(out=dst, in_=src),
                          reads=[b_bank[bank]], writes=[b_hs[stg][gl][kb]])
            if gl == 3 or g == NG - 1:
                ntok = (gl + 1) * 128
                g0 = g - gl
                rb = [b for glb in b_hs[stg][:gl + 1] for b in glb]
                pend_hst.append((g, stg, g0, ntok, rb))

        stage_a(0)
        for g in range(NG):
            if g + 1 < NG:
                stage_a(g + 1)
            stage_b(g)
        emit_hst(NG)
        pro_done = [hst_deps[0], hst_deps[1], ('act', tr.cnt['act']), ('dve', tr.cnt['dve']), ('pe', tr.cnt['pe'])]

        wstate = {'n': 0, 'ti': 0, 'm': 0}
        wb_pending = []
        wb_dep = {}
        first_bf = {}

        def flush_wb(upto):
            while wb_pending and wb_pending[0][0] <= upto:
                n, s, dst, nelem, ckey = wb_pending.pop(0)
                wb_dep[ckey] = tr.dma('sp', lambda e, s=s, dst=dst, nelem=nelem: e.dma_start(out=dst, in_=wslot[s][:, 0:nelem]),
                                      'wb%d' % s, reads=[b_w[s]])

        def wload(src_ap, nelem, ckey, f32_ap):
            ti = wstate['ti']
            n = wstate['n']
            s = n % 4
            wstate['n'] += 1
            flush_wb(n - 3)
            if ckey not in first_bf:
                if ckey[0] == 'in' and ckey[1] // 16 in (SEG_A, SEG_B):
                    first_bf[ckey] = 1
                else:
                    first_bf[ckey] = 1 + wstate['m'] % 2
                    wstate['m'] += 1
            if ti < first_bf[ckey]:
                tr.dma('pool', lambda e, s=s: e.dma_start(out=wslot[s][:, 0:nelem], in_=f32_ap), 'wc%d' % s,
                       writes=[b_w[s]])
                if ti == first_bf[ckey] - 1:
                    wb_pending.append((n, s, src_ap, nelem, ckey))
            else:
                tr.dma('sp', lambda e, s=s: e.dma_start(out=wslot[s][:, 0:nelem], in_=src_ap), 'w%d' % s,
                       writes=[b_w[s]], extra=[wb_dep[ckey]])
            return s

        pstate = {'n': 0}

        def next_banks():
            b = (0, 1) if pstate['n'] % 2 == 0 else (2, 3)
            pstate['n'] += 1
            return b

        def mm_pair(wtiles, KH, rhs_fn, rhs_bufs, banks, N, tile0, extra=()):
            nh = len(wtiles)
            dep = None
            for h, wtile in enumerate(wtiles):
                s = wload(*wtile)
                wt = wslot[s][:, 0:KH * 256].rearrange("p (k n) -> p k n", n=256)

                def fn(e, wt=wt, h=h):
                    last = None
                    for q in range(2):
                        for kc in range(KH):
                            last = e.matmul(PS[:, banks[q], 0:N], lhsT=wt[:, kc, q * 128:(q + 1) * 128],
                                            rhs=rhs_fn(h * KH + kc), start=(h == 0 and kc == 0),
                                            stop=(h == nh - 1 and kc == KH - 1))
                    return last
                dep = tr.op('pe', fn, reads=[b_w[s]] + list(rhs_bufs), writes=[b_bank[banks[0]], b_bank[banks[1]]],
                            extra=extra)
            return dep

        def inproj_pair(seg, pr, banks, halo, tile0, extra=(), Tj=T):
            grp = seg * 16 + pr
            wt = [(wq_in[grp, :, h, :], 4096, ('in', grp, h), w_in[grp, :, h, :]) for h in range(2)]
            if halo:
                return mm_pair(wt, 16, lambda kc: hT[:, kc, 0:Tj + 32], [b_hT], banks, Tj + 32, tile0, extra)
            return mm_pair(wt, 16, lambda kc: hT[:, kc, 16:16 + Tj], [b_hT], banks, Tj, tile0, extra)

        def proj_pair(wq, w_ap, name, pr, rhs_view, rhs_bufs, banks, tile0, Tj=T):
            wt = [(wq[pr, :, h, :], 4096, (name, pr, h), w_ap[pr, :, h, :]) for h in range(2)]
            return mm_pair(wt, 16, lambda kc: rhs_view[:, kc, 0:Tj], rhs_bufs, banks, Tj, tile0)

        prev_tile_done = []
        ring = {'n': 0}
        ystore_deps = []
        deferred = []

        def flush_stores():
            nonlocal ystore_deps
            if not deferred:
                return
            ystore_deps = []
            for (tg, nt, o0) in deferred:
                ystore_deps.append(tr.dma('sp', lambda e, tg=tg, nt=nt, o0=o0: e.dma_start(out=y[o0:o0 + nt, :], in_=ybuf[tg][0:nt, :]),
                                          'ys%d' % tg, reads=[b_y[tg]]))
            del deferred[:]

        for ti in range(ntiles):
            c, j = divmod(ti, NTC)
            base = c * CHR + j * T
            tile0 = (ti == 0)
            wstate['ti'] = ti
            Tj = 384 if j == NTC - 1 else T
            Wj = Tj + 32
            tx = list(pro_done) if tile0 else list(prev_tile_done)
            txg = list(pro_done) if tile0 else []

            def load_hT(dq, base, extra, Wj_):
                tr.dma(dq, lambda e, base=base, Wj_=Wj_, Tj=Tj: e.dma_start(out=hT[:, :, 0:Wj_], in_=hT_all[:, :, base:base + Wj_].rearrange("k p t -> p k t")),
                       'hT', writes=[b_hT], extra=extra)
                tr.dma(dq, lambda e, base=base, Wj_=Wj_, Tj=Tj: e.dma_start(out=mk[:, 0:Wj_], in_=maskb[:, base:base + Wj_]), 'mk', writes=[b_mk])
            if tile0:
                load_hT('sp', base, pro_done, Wj)

            m2, m4, m8, m16 = (tmpv[0][:, 0:Wj - 1], tmpv[1][:, 0:Wj - 3], tmpv[2][:, 0:Wj - 7], tmpv[3][:, 0:Wj - 15])
            tr.op('dve', lambda e, m2=m2, m4=m4, m8=m8, m16=m16, Tj=Tj, Wj=Wj: e.tensor_tensor(out=m2, in0=mk[:, 0:Wj - 1], in1=mk[:, 1:Wj], op=ALU.add),
                  reads=[b_mk], writes=[b_t[0]], extra=txg)
            tr.op('dve', lambda e, m2=m2, m4=m4, m8=m8, m16=m16, Tj=Tj, Wj=Wj: e.tensor_tensor(out=m4, in0=m2[:, 0:Wj - 3], in1=m2[:, 2:Wj - 1], op=ALU.add),
                  reads=[b_t[0]], writes=[b_t[1]], extra=txg)
            tr.op('dve', lambda e, m2=m2, m4=m4, m8=m8, m16=m16, Tj=Tj, Wj=Wj: e.tensor_tensor(out=m8, in0=m4[:, 0:Wj - 7], in1=m4[:, 4:Wj - 3], op=ALU.add),
                  reads=[b_t[1]], writes=[b_t[2]], extra=txg)
            tr.op('dve', lambda e, m2=m2, m4=m4, m8=m8, m16=m16, Tj=Tj, Wj=Wj: e.tensor_tensor(out=m16, in0=m8[:, 0:Wj - 15], in1=m8[:, 8:Wj - 7], op=ALU.add),
                  reads=[b_t[2]], writes=[b_t[3]], extra=txg)
            for gi, (srcm, off) in enumerate(((m2, 15), (m4, 14), (m8, 12), (m16, 8))):
                tr.op('dve', lambda e, gi=gi, srcm=srcm, off=off, Tj=Tj, Wj=Wj: e.tensor_scalar(
                    out=inv[gi][:, 0:Tj], in0=srcm[:, off:off + Tj], scalar1=1.0, scalar2=None, op0=ALU.max),
                    reads=[b_t[gi]], writes=[b_inv[gi]])
                tr.op('dve', lambda e, gi=gi, Tj=Tj, Wj=Wj: e.reciprocal(out=inv[gi][:, 0:Tj], in_=inv[gi][:, 0:Tj]), reads=[b_inv[gi]], writes=[b_inv[gi]])

            pending_stats = []

            def flush_stats(upto):
                while pending_stats and pending_stats[0] <= upto:
                    i = pending_stats.pop(0)
                    qq = i % 2
                    tr.op('pe', lambda e, i=i, Tj=Tj, Wj=Wj: e.matmul(PS[:, 6, 0:Tj], lhsT=ones, rhs=cbuf[:, i, 0:Tj], start=(i == 0), stop=(i == KC - 1)),
                          reads=[b_c[i], b_const], writes=[b_bank[6]])
                    tr.op('pe', lambda e, i=i, qq=qq, Tj=Tj, Wj=Wj: e.matmul(PS[:, 7, 0:Tj], lhsT=ones, rhs=tmpv[6 + qq][:, 0:Tj], start=(i == 0), stop=(i == KC - 1)),
                          reads=[b_t[6 + qq], b_const], writes=[b_bank[7]])

            def emit_sq(ip):
                for q in range(2):
                    i = 2 * ip + q
                    tr.op('act', lambda e, i=i, q=q, Tj=Tj, Wj=Wj: e.activation(out=tmpv[6 + q][:, 0:Tj], in_=cbuf[:, i, 0:Tj], func=AF.Square),
                          reads=[b_c[i]], writes=[b_t[6 + q]])
                    pending_stats.append(i)

            for ip in range(16):
                bb = next_banks()
                inproj_pair(SEG_B, ip, bb, True, tile0, extra=txg, Tj=Tj)
                for q in range(2):
                    tr.op('act', lambda e, q=q, bb=bb, Tj=Tj, Wj=Wj: e.activation(out=tmpv[q][:, 0:Wj], in_=PS[:, bb[q], 0:Wj], func=AF.Sigmoid),
                          reads=[b_bank[bb[q]]], writes=[b_t[q]], extra=txg)
                if ip > 0:
                    emit_sq(ip - 1)
                ab = next_banks()
                inproj_pair(SEG_A, ip, ab, True, tile0, extra=txg, Tj=Tj)
                if ip == 0:
                    flush_stores()
                    tx = tx + ystore_deps
                flush_stats(2 * ip - 1)
                vs = [2 + (2 * ip + q) % 4 for q in range(2)]
                for q in range(2):
                    tr.op('dve', lambda e, q=q, ab=ab, vs=vs, Tj=Tj, Wj=Wj: e.tensor_tensor(out=tmpv[vs[q]][:, 0:Wj], in0=PS[:, ab[q], 0:Wj], in1=tmpv[q][:, 0:Wj], op=ALU.mult),
                          reads=[b_bank[ab[q]], b_t[q]], writes=[b_t[vs[q]]], extra=txg)
                npool = 0 if ti < 1 else NPOOL
                kd = 31 - npool
                if npool:
                    for k in range(kd, 31):
                        for q in range(2):
                            i = 2 * ip + q
                            vin = tmpv[vs[q]][:, k + 1:k + 1 + Tj]
                            ppv = tmpv[12 + q][:, 0:Tj]
                            if k == kd:
                                tr.op('act', lambda e, ppv=ppv, vin=vin, i=i, k=k: e.activation(
                                    out=ppv, in_=vin, func=AF.Identity, scale=cw[:, i, k:k + 1]),
                                    reads=[b_t[vs[q]], b_const], writes=[b_t[12 + q]], extra=tx)
                            else:
                                r = 14 + ring['n'] % 3
                                ring['n'] += 1
                                ptv = tmpv[r][:, 0:Tj]
                                tr.op('act', lambda e, ptv=ptv, vin=vin, i=i, k=k: e.activation(
                                    out=ptv, in_=vin, func=AF.Identity, scale=cw[:, i, k:k + 1]),
                                    reads=[b_t[vs[q]], b_const], writes=[b_t[r]], extra=tx)
                                tr.op('pool', lambda e, ppv=ppv, ptv=ptv: e.tensor_tensor(out=ppv, in0=ppv, in1=ptv, op=ALU.add),
                                      reads=[b_t[12 + q], b_t[r]], writes=[b_t[12 + q]])
                for k in range(kd):
                    for q in range(2):
                        i = 2 * ip + q
                        acc = PS[:, 4 + q, 0:Tj]
                        vin = tmpv[vs[q]][:, k + 1:k + 1 + Tj]
                        if k == 0:
                            tr.op('dve', lambda e, acc=acc, vin=vin, i=i, Tj=Tj, Wj=Wj: e.tensor_scalar(
                                out=acc, in0=vin, scalar1=cw[:, i, 0:1], scalar2=pcol(PV_CB, i), op0=ALU.mult, op1=ALU.add),
                                reads=[b_t[vs[q]], b_const], writes=[b_bank[4 + q]], extra=txg)
                        elif k < 30:
                            tr.op('dve', lambda e, acc=acc, vin=vin, i=i, k=k, Tj=Tj, Wj=Wj: e.scalar_tensor_tensor(
                                out=acc, in0=vin, scalar=cw[:, i, k:k + 1], in1=acc, op0=ALU.mult, op1=ALU.add),
                                reads=[b_t[vs[q]], b_bank[4 + q]], writes=[b_bank[4 + q]])
                        else:
                            tr.op('dve', lambda e, acc=acc, vin=vin, i=i, k=k, Tj=Tj, Wj=Wj: e.scalar_tensor_tensor(
                                out=cbuf[:, i, 0:Tj], in0=vin, scalar=cw[:, i, k:k + 1], in1=acc, op0=ALU.mult, op1=ALU.add),
                                reads=[b_t[vs[q]], b_bank[4 + q]], writes=[b_c[i]], extra=tx)
                if npool:
                    for q in range(2):
                        i = 2 * ip + q
                        tr.op('dve', lambda e, q=q, i=i, Tj=Tj, Wj=Wj: e.tensor_tensor(out=cbuf[:, i, 0:Tj], in0=PS[:, 4 + q, 0:Tj], in1=tmpv[12 + q][:, 0:Tj], op=ALU.add),
                              reads=[b_bank[4 + q], b_t[12 + q]], writes=[b_c[i]], extra=tx)

            def emit_z(ip):
                zb = next_banks()
                inproj_pair(SEG_ZC, ip, zb, False, tile0, Tj=Tj)
                for q in range(2):
                    szs = 8 + 2 * (ip % 2) + q
                    tr.op('act', lambda e, q=q, zb=zb, szs=szs, Tj=Tj, Wj=Wj: e.activation(out=tmpv[szs][:, 0:Tj], in_=PS[:, zb[q], 0:Tj], func=AF.Silu),
                          reads=[b_bank[zb[q]]], writes=[b_t[szs]], extra=tx)
            emit_z(0)
            emit_z(1)
            emit_sq(15)
            flush_stats(KC)

            tA, tB = tmpv[0][:, 0:Tj], tmpv[1][:, 0:Tj]
            tr.op('dve', lambda e, tA=tA, tB=tB, Tj=Tj, Wj=Wj: e.tensor_scalar(out=tA, in0=PS[:, 6, 0:Tj], scalar1=1.0 / D, scalar2=None, op0=ALU.mult),
                  reads=[b_bank[6]], writes=[b_t[0]])
            tr.op('dve', lambda e, tA=tA, tB=tB, Tj=Tj, Wj=Wj: e.tensor_tensor(out=tB, in0=tA, in1=tA, op=ALU.mult), reads=[b_t[0]], writes=[b_t[1]])
            tr.op('dve', lambda e, tA=tA, tB=tB, Tj=Tj, Wj=Wj: e.scalar_tensor_tensor(out=tB, in0=PS[:, 7, 0:Tj], scalar=1.0 / D, in1=tB, op0=ALU.mult, op1=ALU.subtract),
                  reads=[b_bank[7], b_t[1]], writes=[b_t[1]])
            tr.op('act', lambda e, tA=tA, tB=tB, Tj=Tj, Wj=Wj: e.activation(out=tB, in_=tB, func=AF.Sqrt, bias=epsl, scale=1.0), reads=[b_t[1], b_eps], writes=[b_t[1]])
            tr.op('dve', lambda e, tA=tA, tB=tB, Tj=Tj, Wj=Wj: e.reciprocal(out=PS[:, 6, 0:Tj], in_=tB), reads=[b_t[1]], writes=[b_bank[6]])
            tr.op('dve', lambda e, tA=tA, tB=tB, Tj=Tj, Wj=Wj: e.scalar_tensor_tensor(out=PS[:, 7, 0:Tj], in0=tA, scalar=-1.0, in1=PS[:, 6, 0:Tj], op0=ALU.mult, op1=ALU.mult),
                  reads=[b_t[0], b_bank[6]], writes=[b_bank[7]])

            for ip in range(16):
                if ip >= 2:
                    emit_z(ip)
                for q in range(2):
                    i = 2 * ip + q
                    tr.op('dve', lambda e, q=q, i=i, Tj=Tj, Wj=Wj: e.tensor_tensor(out=tmpv[4 + q][:, 0:Tj], in0=cbuf[:, i, 0:Tj], in1=PS[:, 6, 0:Tj], op=ALU.mult),
                          reads=[b_c[i], b_bank[6]], writes=[b_t[4 + q]])
                for q in range(2):
                    tr.op('dve', lambda e, q=q, Tj=Tj, Wj=Wj: e.tensor_tensor(out=tmpv[4 + q][:, 0:Tj], in0=tmpv[4 + q][:, 0:Tj], in1=PS[:, 7, 0:Tj], op=ALU.add),
                          reads=[b_t[4 + q], b_bank[7]], writes=[b_t[4 + q]])
                for q in range(2):
                    i = 2 * ip + q
                    tr.op('act', lambda e, q=q, i=i, Tj=Tj, Wj=Wj: e.activation(out=tmpv[6 + q][:, 0:Tj], in_=tmpv[4 + q][:, 0:Tj], func=AF.Silu,
                                                                  scale=pcol(PV_LNG, i), bias=pcol(PV_LNB, i)),
                          reads=[b_t[4 + q], b_const], writes=[b_t[6 + q]])
                for q in range(2):
                    i = 2 * ip + q
                    szs = 8 + 2 * (ip % 2) + q
                    tr.op('dve', lambda e, q=q, i=i, szs=szs, Tj=Tj, Wj=Wj: e.tensor_tensor(out=yc_in[:, i, 0:Tj], in0=tmpv[6 + q][:, 0:Tj], in1=tmpv[szs][:, 0:Tj], op=ALU.mult),
                          reads=[b_t[6 + q], b_t[szs]], writes=[b_yc[i]], extra=tx)
            c_done = [('dve', tr.cnt['dve']), ('pe', tr.cnt['pe']), ('act', tr.cnt['act'])]

            mx = [view(O_TMP + 9 * 1792, [8, T], BF16), view(O_TMP + 13 * 1792, [8, T], BF16)]
            for g in range(4):
                kwin = (2, 4, 8, 16)[g]
                mxi = g % 2
                for up in range(4):
                    ub = next_banks()
                    inproj_pair(SEG_U, g * 4 + up, ub, True, tile0, Tj=Tj)
                    for q in range(2):
                        tr.op('act', lambda e, q=q, ub=ub, Tj=Tj, Wj=Wj: e.activation(out=tmpv[q][:, 0:Wj], in_=PS[:, ub[q], 0:Wj], func=AF.Copy),
                              reads=[b_bank[ub[q]]], writes=[b_t[q]])
                    for q in range(2):
                        tr.op('dve', lambda e, q=q, ub=ub, Tj=Tj, Wj=Wj: e.tensor_tensor(out=tmpv[2 + q][:, 0:Wj - 1], in0=PS[:, ub[q], 0:Wj - 1], in1=tmpv[q][:, 1:Wj], op=ALU.add),
                              reads=[b_bank[ub[q]], b_t[q]], writes=[b_t[2 + q]])
                    cur = [2, 3]
                    off = 15
                    ln = Wj - 1
                    for step, sh in ((4, 2), (8, 4), (16, 8)):
                        if kwin < step:
                            break
                        nxt = [4, 5] if cur[0] == 2 else [2, 3]
                        nl = ln - sh
                        for q in range(2):
                            tr.op('dve', lambda e, q=q, cur=cur, nxt=nxt, nl=nl, sh=sh, Tj=Tj, Wj=Wj: e.tensor_tensor(
                                out=tmpv[nxt[q]][:, 0:nl], in0=tmpv[cur[q]][:, 0:nl], in1=tmpv[cur[q]][:, sh:sh + nl], op=ALU.add),
                                reads=[b_t[cur[q]]], writes=[b_t[nxt[q]]])
                        cur = nxt
                        ln = nl
                        off -= sh // 2
                    for q in range(2):
                        tr.op('dve', lambda e, q=q, cur=cur, off=off, g=g, Tj=Tj, Wj=Wj: e.tensor_tensor(
                            out=tmpv[6 + q][:, 0:Tj], in0=tmpv[cur[q]][:, off:off + Tj], in1=inv[g][:, 0:Tj], op=ALU.mult),
                            reads=[b_t[cur[q]], b_inv[g]], writes=[b_t[6 + q]])
                    for q in range(2):
                        tr.op('dve', lambda e, q=q, up=up, mxi=mxi, Tj=Tj, Wj=Wj: e.tensor_tensor(
                            out=mx[mxi][:, 2 * up + q, 0:Tj], in0=tmpv[6 + q][:, 0:Tj], in1=tmpv[q][:, 16:16 + Tj], op=ALU.subtract),
                            reads=[b_t[6 + q], b_t[q]], writes=[b_mxc[mxi][2 * up + q]], extra=tx)
                for opi in range(4):
                    zb = next_banks()
                    inproj_pair(SEG_ZP, g * 4 + opi, zb, False, tile0, Tj=Tj)
                    for q in range(2):
                        tr.op('act', lambda e, q=q, zb=zb, Tj=Tj, Wj=Wj: e.activation(out=tmpv[7 - q][:, 0:Tj], in_=PS[:, zb[q], 0:Tj], func=AF.Silu),
                              reads=[b_bank[zb[q]]], writes=[b_t[7 - q]])
                    wb = next_banks()
                    mm_pair([(wq_pool[g, opi], 2048, ('pool', g, opi), w_pool[g, opi])], 8,
                            lambda kc, mxi=mxi, Tj=Tj: mx[mxi][:, kc, 0:Tj], b_mxc[mxi], wb, Tj, tile0)
                    for q in range(2):
                        i = g * 8 + 2 * opi + q
                        tr.op('dve', lambda e, q=q, i=i, wb=wb, Tj=Tj, Wj=Wj: e.scalar_tensor_tensor(
                            out=yp_in[:, i, 0:Tj], in0=PS[:, wb[q], 0:Tj], scalar=pcol(PV_LS, i), in1=tmpv[7 - q][:, 0:Tj], op0=ALU.mult, op1=ALU.mult),
                            reads=[b_bank[wb[q]], b_t[7 - q], b_const], writes=[b_yp[i]], extra=c_done + tx)

            for dp in range(16):
                gb = next_banks()
                inproj_pair(SEG_GP, dp, gb, False, tile0, Tj=Tj)
                for q in range(2):
                    tr.op('act', lambda e, q=q, gb=gb, Tj=Tj, Wj=Wj: e.activation(out=tmpv[q][:, 0:Tj], in_=PS[:, gb[q], 0:Tj], func=AF.Sigmoid),
                          reads=[b_bank[gb[q]]], writes=[b_t[q]])
                gcb = next_banks()
                inproj_pair(SEG_GC, dp, gcb, False, tile0, Tj=Tj)
                for q in range(2):
                    tr.op('act', lambda e, q=q, gcb=gcb, Tj=Tj, Wj=Wj: e.activation(out=tmpv[2 + q][:, 0:Tj], in_=PS[:, gcb[q], 0:Tj], func=AF.Sigmoid),
                          reads=[b_bank[gcb[q]]], writes=[b_t[2 + q]])
                pb = next_banks()
                proj_pair(wq_pp, w_pp, 'pp', dp, yp_in, b_yp, pb, tile0, Tj=Tj)
                for q in range(2):
                    tr.op('dve', lambda e, q=q, pb=pb, Tj=Tj, Wj=Wj: e.tensor_tensor(out=tmpv[4 + q][:, 0:Tj], in0=PS[:, pb[q], 0:Tj], in1=tmpv[q][:, 0:Tj], op=ALU.mult),
                          reads=[b_bank[pb[q]], b_t[q]], writes=[b_t[4 + q]])
                cb_ = next_banks()
                proj_pair(wq_pc, w_pc, 'pc', dp, yc_in, b_yc, cb_, tile0, Tj=Tj)
                for q in range(2):
                    i = 2 * dp + q
                    tr.op('dve', lambda e, q=q, i=i, cb_=cb_, Tj=Tj, Wj=Wj: e.scalar_tensor_tensor(
                        out=tmpv[6 + q][:, 0:Tj], in0=PS[:, cb_[q], 0:Tj], scalar=pcol(PV_BPC, i), in1=tmpv[2 + q][:, 0:Tj], op0=ALU.add, op1=ALU.mult),
                        reads=[b_bank[cb_[q]], b_t[2 + q], b_const], writes=[b_t[6 + q]])
                for q in range(2):
                    i = 2 * dp + q
                    tr.op('dve', lambda e, q=q, i=i, Tj=Tj, Wj=Wj: e.tensor_tensor(out=merged[:, i, 0:Tj], in0=tmpv[4 + q][:, 0:Tj], in1=tmpv[6 + q][:, 0:Tj], op=ALU.add),
                          reads=[b_t[4 + q], b_t[6 + q]], writes=[b_mg[i]], extra=c_done)
            p4_done = [('pe', tr.cnt['pe']), ('dve', tr.cnt['dve']), ('act', tr.cnt['act'])]

            if ti + 1 < ntiles:
                c2, j2 = divmod(ti + 1, NTC)
                load_hT('sp', c2 * CHR + j2 * T, [], (384 if j2 == NTC - 1 else T) + 32)

            ntoks = [128, 128, 128, 32]
            ngrp = 3 if j == NTC - 1 else 4
            ssq = [sm[:, 128 + tg * 8:128 + tg * 8 + 8] for tg in range(4)]
            for dblk in range(8):
                bset = (dblk % 2) * 4
                for kq in range(4):
                    s = wload(wq_out[dblk, :, kq, :], 4096, ('out', dblk, kq), w_out[dblk, :, kq, :])
                    wt = wslot[s].rearrange("p (k n) -> p k n", n=512)

                    def fn(e, wt=wt, kq=kq, bset=bset, ngrp=ngrp):
                        last = None
                        for tg in range(ngrp):
                            nt = ntoks[tg]
                            for kc in range(8):
                                last = e.matmul(PS[0:nt, bset + tg, :], lhsT=merged[:, kq * 8 + kc, tg * 128:tg * 128 + nt],
                                                rhs=wt[:, kc, :], start=(kq == 0 and kc == 0), stop=(kq == 3 and kc == 7))
                        return last
                    tr.op('pe', fn, reads=[b_w[s]] + b_mg, writes=[b_bank[bset + tg] for tg in range(ngrp)])
                if dblk == 0:
                  for tg in range(ngrp):
                    r0 = base + 16 + tg * 128
                    nt = ntoks[tg]
                    tr.dma('sp', lambda e, tg=tg, r0=r0, nt=nt, Tj=Tj, Wj=Wj: e.dma_start(out=ybuf[tg][0:nt, :], in_=x_in[r0:r0 + nt, :]),
                           'yl%d' % tg, writes=[b_y[tg]], extra=p4_done + ystore_deps)
                for tg in range(ngrp):
                    nt = ntoks[tg]
                    ysl = ybuf[tg][0:nt, dblk * 512:(dblk + 1) * 512]
                    tr.op('dve', lambda e, tg=tg, nt=nt, ysl=ysl, bset=bset, Tj=Tj, Wj=Wj: e.tensor_tensor(out=ysl, in0=PS[0:nt, bset + tg, :], in1=ysl, op=ALU.add),
                          reads=[b_bank[bset + tg], b_y[tg]], writes=[b_y[tg]])
                    tr.op('act', lambda e, tg=tg, nt=nt, ysl=ysl, dblk=dblk, Tj=Tj, Wj=Wj: e.activation(
                        out=junkv[tg % 2][0:nt, :], in_=ysl, func=AF.Square, accum_out=ssq[tg][0:nt, dblk:dblk + 1]),
                        reads=[b_y[tg]], writes=[b_t[4 + 2 * (tg % 2)], b_t[5 + 2 * (tg % 2)], b_sm[tg * 8 + dblk]], extra=p4_done)
            for tg in range(ngrp):
                nt = ntoks[tg]
                tot = sm[0:nt, 192 + tg:193 + tg]
                tr.op('dve', lambda e, tg=tg, nt=nt, tot=tot, Tj=Tj, Wj=Wj: e.tensor_reduce(out=tot, in_=ssq[tg][0:nt, :], axis=mybir.AxisListType.X, op=ALU.add),
                      reads=[b_sm[tg * 8 + d_] for d_ in range(8)], writes=[b_sm[40 + tg]])
                tr.op('act', lambda e, nt=nt, tot=tot, Tj=Tj, Wj=Wj: e.activation(out=tot, in_=tot, func=AF.Sqrt, scale=1.0 / D, bias=epsr[0:nt, :]),
                      reads=[b_sm[40 + tg], b_eps], writes=[b_sm[40 + tg]])
                tr.op('dve', lambda e, tot=tot, Tj=Tj, Wj=Wj: e.reciprocal(out=tot, in_=tot), reads=[b_sm[40 + tg]], writes=[b_sm[40 + tg]])
                tr.op('dve', lambda e, tg=tg, nt=nt, tot=tot, Tj=Tj, Wj=Wj: e.scalar_tensor_tensor(
                    out=ybuf[tg][0:nt, :], in0=ybuf[tg][0:nt, :], scalar=tot, in1=gf[0:nt, :], op0=ALU.mult, op1=ALU.mult),
                    reads=[b_y[tg], b_sm[40 + tg], b_gf], writes=[b_y[tg]])
                o0 = c * 2048 + j * T + tg * 128
                deferred.append((tg, nt, o0))
            prev_tile_done = [('pe', tr.cnt['pe']), ('dve', tr.cnt['dve']), ('act', tr.cnt['act'])]
            flush_wb(1 << 60)

        flush_stores()
        tr.wait('sp', ystore_deps + [(k, v) for k, v in tr.dcnt.items() if k.startswith('ys') or k.startswith('wb')])

        for k in list(tr.dcnt.keys()):
            sem(k)
        with nc.Block() as block:
            def run(eng_obj, key):
                for o in tr.ops[key]:
                    if o[0] == 'w':
                        eng_obj.wait_ge(sem(o[1]), o[2])
                    elif o[0] == 'o':
                        o[1](eng_obj).then_inc(sem(key), 1)
                    else:
                        o[1](eng_obj).then_inc(sem(o[2]), 16)

            @block.sync
            def _(e):
                run(e, 'sp')

            @block.gpsimd
            def _(e):
                run(e, 'pool')

            @block.tensor
            def _(e):
                run(e, 'pe')

            @block.scalar
            def _(e):
                run(e, 'act')

            @block.vector
            def _(e):
                run(e, 'dve')
    return nc


def _prep_core(i, xp, xsm):
    xin = np.zeros((ROWS, D), np.float32)
    mask = np.zeros((ROWS,), np.float32)
    for c in range(NCH):
        if c == 0:
            seq, s0 = xp[i], 0
        else:
            seq, s0 = xsm[i // 2], (i % 2) * 2048
        L = seq.shape[0]
        lo = s0 - 16
        hi = s0 + CHR - 16
        a, b = max(lo, 0), min(hi, L)
        xin[c * CHR + (a - lo): c * CHR + (b - lo)] = seq[a:b]
        mask[c * CHR + (a - lo): c * CHR + (b - lo)] = 1.0
    return xin, np.ascontiguousarray(np.broadcast_to(mask[None, :], (128, ROWS)))


_NC_CACHE = {}


def _prep_shared(g_norm, w_in, w_pool, ls_pool, w_proj_pool, conv_w, conv_b, ln_g, ln_b, w_proj_conv, b_proj_conv,
                 w_out, g_final):
    f = lambda a: np.ascontiguousarray(np.asarray(a, dtype=np.float32))
    pvec = np.zeros((128, 160), np.float32)
    for jv, vec in enumerate((ls_pool, conv_b, ln_g, ln_b, b_proj_conv)):
        pvec[:, jv * 32:(jv + 1) * 32] = f(vec).reshape(32, 128).T
    cwp = np.ascontiguousarray(f(conv_w).reshape(31, 32, 128).transpose(2, 1, 0).reshape(128, 992))
    cst = np.concatenate([np.ones((128, 128), np.float32), np.eye(128, dtype=np.float32)], axis=1)
    blk = lambda w, ng: np.ascontiguousarray(w.reshape(2, 16, 128, ng, 256).transpose(3, 2, 0, 1, 4)).reshape(ng, 128, 2, 4096)
    return {
        "w_in": blk(f(w_in).reshape(D, NIN), 112),
        "w_pool": np.ascontiguousarray(f(w_pool).reshape(4, 8, 128, 4, 256).transpose(0, 3, 2, 1, 4)).reshape(4, 4, 128, 2048),
        "w_pp": blk(f(w_proj_pool).reshape(D, D), 16), "w_pc": blk(f(w_proj_conv).reshape(D, D), 16),
        "w_out": np.ascontiguousarray(f(w_out).reshape(4, 8, 128, 8, 512).transpose(3, 2, 0, 1, 4)).reshape(8, 128, 4, 4096),
        "gnb": np.ascontiguousarray(np.broadcast_to(f(g_norm).reshape(1, D), (128, D))),
        "gfb": np.ascontiguousarray(np.broadcast_to(f(g_final).reshape(1, D), (128, D))),
        "pvec": pvec, "cwp": cwp, "cst": cst,
    }


def kernel(x_prompt, x_sample, g_norm, w_in, w_pool, ls_pool, w_proj_pool, conv_w, conv_b, ln_g, ln_b,
           w_proj_conv, b_proj_conv, w_out, g_final):
    f = lambda a: np.ascontiguousarray(np.asarray(a, dtype=np.float32))
    xp, xsm = f(x_prompt), f(x_sample)
    shared = _prep_shared(g_norm, w_in, w_pool, ls_pool, w_proj_pool, conv_w, conv_b, ln_g, ln_b, w_proj_conv,
                          b_proj_conv, w_out, g_final)
    in_maps = []
    for i in range(8):
        xin, mb = _prep_core(i, xp, xsm)
        m = dict(shared)
        m["x_in"] = xin
        m["maskb"] = mb
        in_maps.append(m)
    if "nc" not in _NC_CACHE:
        _NC_CACHE["nc"] = build_nc()
    res = run_bass_kernel_spmd(_NC_CACHE["nc"], in_maps, core_ids=list(range(8)))
    y_prompt = np.empty((8, 2048, D), np.float32)
    y_sample = np.empty((4, 4096, D), np.float32)
    for i in range(8):
        yo = np.asarray(res.results[i]["y"], dtype=np.float32).reshape(NCH * 2048, D)
        y_prompt[i] = yo[0:2048]
        s0 = (i % 2) * 2048
        y_sample[i // 2, s0:s0 + 2048] = yo[2048:4096]
    return (y_prompt, y_sample)
```
